# Optimizing a Trainium2 kernel written in Bass

```python
import jax
import jax.numpy as jnp
from jax import lax
import numpy as np

D_MODEL = 2048
BATCH = 2
SEQ = 4096
DEPTH = 4
DEC_BATCH = 8
DEC_SEQ = 1
PAST_LEN = 16384
PAGE_SIZE = 128

HEAD_DIM = 128
BRANCH_WIDTH = D_MODEL // 4
POOL_WIDTH = BRANCH_WIDTH
POOL_WINDOWS = (2, 4, 8, 16)
POOL_GDIM = POOL_WIDTH // len(POOL_WINDOWS)
POOL_BUF = max(POOL_WINDOWS) - 1
DIL_PAIRS = ((128, 1), (512, 4), (2048, 16))
N_DIL = len(DIL_PAIRS)
DIL_HEADS = BRANCH_WIDTH // HEAD_DIM
DIL_WIDTH = DIL_HEADS * HEAD_DIM
MEM_LEN = 256
MEM_HEADS = 4
MEM_WIDTH = MEM_HEADS * HEAD_DIM
MIX_WIDTH = POOL_WIDTH + DIL_WIDTH + MEM_WIDTH
IN_COLS = 2 * POOL_WIDTH + 3 * N_DIL * DIL_WIDTH + DIL_WIDTH + 2 * MEM_WIDTH
BAND_BLOCK = 128
EPS = 1e-6
ATTN_SCALE = HEAD_DIM ** -0.5

kernel_name = 'hybrid_pool_dilated_memory_step'


def rms_norm(x, g):
    xf = x.astype(jnp.float32)
    y = xf * lax.rsqrt(jnp.mean(xf * xf, axis=-1, keepdims=True) + EPS)
    return (y * g.astype(jnp.float32)).astype(x.dtype)


def pool_mix(u_ext, pos0, pool_w, pool_scale):
    B, E, C = u_ext.shape
    L = E - POOL_BUF
    uf = u_ext.astype(jnp.float32)
    cs = jnp.concatenate([jnp.zeros((B, 1, C), jnp.float32), jnp.cumsum(uf, axis=1)], axis=1)
    end = cs[:, POOL_BUF + 1:]
    u_new = uf[:, POOL_BUF:]
    pos = pos0 + jnp.arange(L)
    outs = []
    for gi, w in enumerate(POOL_WINDOWS):
        sl = slice(gi * POOL_GDIM, (gi + 1) * POOL_GDIM)
        start = cs[:, POOL_BUF + 1 - w: POOL_BUF + 1 - w + L, sl]
        cnt = jnp.minimum(w, pos + 1).astype(jnp.float32)[None, :, None]
        d = (end[..., sl] - start) / cnt - u_new[..., sl]
        outs.append(jnp.einsum('blc,cd->bld', d, pool_w[gi].astype(jnp.float32)))
    y = jnp.concatenate(outs, axis=-1) * pool_scale.astype(jnp.float32)
    return y.astype(u_ext.dtype)


def band_attention(q, k, v, back):
    N, T, H, Dh = q.shape
    nb = -(-T // BAND_BLOCK)
    Tp = nb * BAND_BLOCK
    qb = jnp.pad(q, ((0, 0), (0, Tp - T), (0, 0), (0, 0))).reshape(N, nb, BAND_BLOCK, H, Dh)
    kpad = ((0, 0), (BAND_BLOCK, Tp - T), (0, 0), (0, 0))
    kp = jnp.pad(k, kpad).reshape(N, nb + 1, BAND_BLOCK, H, Dh)
    vp = jnp.pad(v, kpad).reshape(N, nb + 1, BAND_BLOCK, H, Dh)
    kb = jnp.concatenate([kp[:, :-1], kp[:, 1:]], axis=2)
    vb = jnp.concatenate([vp[:, :-1], vp[:, 1:]], axis=2)
    s = jnp.einsum('nbqhd,nbkhd->nbhqk', qb, kb, preferred_element_type=jnp.float32) * ATTN_SCALE
    qi = BAND_BLOCK + jnp.arange(BAND_BLOCK)[:, None]
    kj = jnp.arange(2 * BAND_BLOCK)[None, :]
    dist = qi - kj
    kabs = (jnp.arange(nb)[:, None, None] - 1) * BAND_BLOCK + kj[None]
    valid = (dist >= 0) & (dist <= back) & (kabs >= 0)
    s = jnp.where(valid[None, :, None], s, -jnp.inf)
    lse = jax.nn.logsumexp(s, axis=-1)
    p = jnp.exp(s - lse[..., None])
    o = jnp.einsum('nbhqk,nbkhd->nbqhd', p.astype(vb.dtype), vb)
    o = o.reshape(N, Tp, H, Dh)[:, :T]
    lse = lse.transpose(0, 1, 3, 2).reshape(N, Tp, H)[:, :T]
    return o, lse


def dilated_prompt(q, k, v, window, dil):
    B, L, H, Dh = q.shape
    T = L // dil
    def to_sub(a):
        return a.reshape(B, T, dil, H, Dh).transpose(0, 2, 1, 3, 4).reshape(B * dil, T, H, Dh)
    o, lse = band_attention(to_sub(q), to_sub(k), to_sub(v), window // dil)
    o = o.reshape(B, dil, T, H, Dh).transpose(0, 2, 1, 3, 4).reshape(B, L, H, Dh)
    lse = lse.reshape(B, dil, T, H).transpose(0, 2, 1, 3).reshape(B, L, H)
    return o, lse


def dilated_sample(q, kv_ext, window, dil, pos0):
    S = q.shape[1]
    back = window // dil
    idx = window + jnp.arange(S)[:, None] - dil * jnp.arange(back + 1)[None, :]
    kg = kv_ext[:, :, 0][:, idx]
    vg = kv_ext[:, :, 1][:, idx]
    s = jnp.einsum('bshd,bskhd->bhsk', q, kg, preferred_element_type=jnp.float32) * ATTN_SCALE
    valid = (pos0 - window + idx) >= 0
    s = jnp.where(valid[None, None], s, -jnp.inf)
    lse = jax.nn.logsumexp(s, axis=-1)
    p = jnp.exp(s - lse[..., None])
    o = jnp.einsum('bhsk,bskhd->bshd', p.astype(vg.dtype), vg)
    return o, lse.transpose(0, 2, 1)


def combine_by_denominator(outs, lses):
    wgt = jax.nn.softmax(jnp.stack(lses, axis=0).astype(jnp.float32), axis=0)
    return jnp.einsum('gblh,gblhd->blhd', wgt, jnp.stack(outs, axis=0).astype(jnp.float32))


def memory_kv(mem, mem_norm_g, w_mem_kv, mem_k_norm):
    B, M, _ = mem.shape
    kv = jnp.einsum('bmd,dc->bmc', rms_norm(mem, mem_norm_g), w_mem_kv).reshape(B, M, 2, MEM_HEADS, HEAD_DIM)
    k = rms_norm(kv[:, :, 0], mem_k_norm)
    return jnp.stack([k, kv[:, :, 1]], axis=2)


def mem_attention(q, kv):
    s = jnp.einsum('blhd,bmhd->bhlm', q, kv[:, :, 0], preferred_element_type=jnp.float32) * ATTN_SCALE
    p = jax.nn.softmax(s, axis=-1)
    return jnp.einsum('bhlm,bmhd->blhd', p.astype(q.dtype), kv[:, :, 1])


def mix_layer(x, pool_prev, dil_prev, mem_kv, pos0, norm_g, w_in, pool_w, pool_scale,
              dil_q_norm, dil_k_norm, mem_q_norm, w_out):
    B, L, _ = x.shape
    z = jnp.einsum('bld,dc->blc', rms_norm(x, norm_g), w_in)
    c0 = 2 * POOL_WIDTH
    c1 = c0 + 3 * N_DIL * DIL_WIDTH
    c2 = c1 + DIL_WIDTH
    c3 = c2 + MEM_WIDTH
    u = z[..., :POOL_WIDTH]
    gate_pool = z[..., POOL_WIDTH:c0]
    qkv = z[..., c0:c1].reshape(B, L, N_DIL, 3, DIL_HEADS, HEAD_DIM)
    gate_dil = z[..., c1:c2]
    q_mem = z[..., c2:c3].reshape(B, L, MEM_HEADS, HEAD_DIM)
    gate_mem = z[..., c3:]

    u_ext = jnp.concatenate([pool_prev.astype(u.dtype), u], axis=1)
    y_pool = pool_mix(u_ext, pos0, pool_w, pool_scale)
    new_pool = u_ext[:, -POOL_BUF:]

    outs, lses, new_dil = [], [], []
    for gi, (win, dil) in enumerate(DIL_PAIRS):
        q = rms_norm(qkv[:, :, gi, 0], dil_q_norm[gi])
        k = rms_norm(qkv[:, :, gi, 1], dil_k_norm[gi])
        v = qkv[:, :, gi, 2]
        kv = jnp.stack([k, v], axis=2)
        if dil_prev is None:
            o, lse = dilated_prompt(q, k, v, win, dil)
            kv_ext = kv if L >= win else jnp.pad(kv, ((0, 0), (win - L, 0), (0, 0), (0, 0), (0, 0)))
        else:
            kv_ext = jnp.concatenate([dil_prev[gi].astype(kv.dtype), kv], axis=1)
            o, lse = dilated_sample(q, kv_ext, win, dil, pos0)
        outs.append(o)
        lses.append(lse)
        new_dil.append(kv_ext[:, -win:])
    y_dil = combine_by_denominator(outs, lses).reshape(B, L, DIL_WIDTH).astype(x.dtype)

    q_mem = rms_norm(q_mem, mem_q_norm)
    y_mem = mem_attention(q_mem, mem_kv.astype(q_mem.dtype)).reshape(B, L, MEM_WIDTH)

    y = jnp.concatenate([jax.nn.silu(gate_pool) * y_pool,
                         jax.nn.silu(gate_dil) * y_dil,
                         jax.nn.silu(gate_mem) * y_mem], axis=-1)
    x = x + jnp.einsum('blc,cd->bld', y, w_out).astype(x.dtype)
    return x, new_pool, new_dil


def setup_inputs(seed: int = 0) -> dict:
    key = jax.random.key(seed)
    ks = jax.random.split(key, 24)
    f32 = jnp.float32

    def nrm(k, shape, scale=1.0):
        return jax.random.normal(k, shape, f32) * scale

    def gain(k, shape):
        return 1.0 + 0.02 * jax.random.normal(k, shape, f32)

    inp = {}
    inp['x_prompt'] = nrm(ks[0], (BATCH, SEQ, D_MODEL))
    inp['x_sample'] = nrm(ks[1], (DEC_BATCH, DEC_SEQ, D_MODEL))
    inp['state_pool'] = nrm(ks[2], (DEPTH, DEC_BATCH, POOL_BUF, POOL_WIDTH))
    inp['cache_dil_w128'] = nrm(ks[3], (DEPTH, DEC_BATCH, DIL_PAIRS[0][0], 2, DIL_HEADS, HEAD_DIM))
    inp['cache_dil_w512'] = nrm(ks[4], (DEPTH, DEC_BATCH, DIL_PAIRS[1][0], 2, DIL_HEADS, HEAD_DIM))
    inp['cache_dil_w2048'] = nrm(ks[5], (DEPTH, DEC_BATCH, DIL_PAIRS[2][0], 2, DIL_HEADS, HEAD_DIM))
    inp['cache_mem_kv'] = nrm(ks[6], (DEPTH, DEC_BATCH, MEM_LEN, 2, MEM_HEADS, HEAD_DIM))
    inp['mem_prompt'] = nrm(ks[7], (BATCH, MEM_LEN, D_MODEL))
    inp['norm_g'] = gain(ks[8], (DEPTH, D_MODEL))
    inp['w_in'] = nrm(ks[9], (DEPTH, D_MODEL, IN_COLS), D_MODEL ** -0.5)
    inp['pool_w'] = nrm(ks[10], (DEPTH, len(POOL_WINDOWS), POOL_GDIM, POOL_GDIM), POOL_GDIM ** -0.5)
    inp['pool_scale'] = gain(ks[11], (DEPTH, POOL_WIDTH))
    inp['dil_q_norm'] = gain(ks[12], (DEPTH, N_DIL, HEAD_DIM))
    inp['dil_k_norm'] = gain(ks[13], (DEPTH, N_DIL, HEAD_DIM))
    inp['mem_norm_g'] = gain(ks[14], (DEPTH, D_MODEL))
    inp['w_mem_kv'] = nrm(ks[15], (DEPTH, D_MODEL, 2 * MEM_WIDTH), D_MODEL ** -0.5)
    inp['mem_q_norm'] = gain(ks[16], (DEPTH, HEAD_DIM))
    inp['mem_k_norm'] = gain(ks[17], (DEPTH, HEAD_DIM))
    inp['w_out'] = nrm(ks[18], (DEPTH, MIX_WIDTH, D_MODEL), MIX_WIDTH ** -0.5)
    return inp


def reference(x_prompt, x_sample, state_pool, cache_dil_w128, cache_dil_w512, cache_dil_w2048,
              cache_mem_kv, mem_prompt, norm_g, w_in, pool_w, pool_scale, dil_q_norm, dil_k_norm,
              mem_norm_g, w_mem_kv, mem_q_norm, mem_k_norm, w_out):
    dil_caches = (cache_dil_w128, cache_dil_w512, cache_dil_w2048)
    y_prompt, y_sample = x_prompt, x_sample
    pool_p, pool_s, mem_p = [], [], []
    dil_p = [[] for _ in DIL_PAIRS]
    dil_s = [[] for _ in DIL_PAIRS]
    pool_zero = jnp.zeros((x_prompt.shape[0], POOL_BUF, POOL_WIDTH), x_prompt.dtype)
    for l in range(DEPTH):
        lw = (norm_g[l], w_in[l], pool_w[l], pool_scale[l], dil_q_norm[l], dil_k_norm[l],
              mem_q_norm[l], w_out[l])
        mem_kv_p = memory_kv(mem_prompt, mem_norm_g[l], w_mem_kv[l], mem_k_norm[l])
        y_prompt, np_l, nd_l = mix_layer(y_prompt, pool_zero, None, mem_kv_p, 0, *lw)
        y_sample, ns_l, nds_l = mix_layer(y_sample, state_pool[l], [c[l] for c in dil_caches],
                                          cache_mem_kv[l], PAST_LEN, *lw)
        pool_p.append(np_l)
        pool_s.append(ns_l)
        mem_p.append(mem_kv_p)
        for gi in range(N_DIL):
            dil_p[gi].append(nd_l[gi])
            dil_s[gi].append(nds_l[gi])
    state_pool_prompt = jnp.stack(pool_p)
    cache_dil_w128_prompt = jnp.stack(dil_p[0])
    cache_dil_w512_prompt = jnp.stack(dil_p[1])
    cache_dil_w2048_prompt = jnp.stack(dil_p[2])
    cache_mem_kv_prompt = jnp.stack(mem_p)
    state_pool_sample = jnp.stack(pool_s)
    cache_dil_w128_sample = jnp.stack(dil_s[0])
    cache_dil_w512_sample = jnp.stack(dil_s[1])
    cache_dil_w2048_sample = jnp.stack(dil_s[2])
    return (y_prompt, y_sample, state_pool_prompt, cache_dil_w128_prompt, cache_dil_w512_prompt,
            cache_dil_w2048_prompt, cache_mem_kv_prompt, state_pool_sample, cache_dil_w128_sample,
            cache_dil_w512_sample, cache_dil_w2048_sample)
```

```python
import numpy as np
import ml_dtypes
from contextlib import ExitStack
import concourse.bass as bass
import concourse.mybir as mybir
from concourse.bass_utils import run_bass_kernel_spmd

F32, BF16 = mybir.dt.float32, mybir.dt.bfloat16
AF = mybir.ActivationFunctionType
ALU = mybir.AluOpType
AX = mybir.AxisListType

DEPTH = 4
D = 2048
L = 4096
TT = 512
NTT = 8
NCH = 16
SLOTC = 1792
NS = 4
SW = 16
WINS = (128, 512, 2048)
DILS = (1, 4, 16)
EPS = 1e-6
SCALE = 128 ** -0.5
GROUPS = [[0, 1, 2, 3], [4, 5, 6, 7]]
V_NG, V_MG, V_PS, V_QN, V_KN, V_MQ, VL = 0, 16, 32, 33, 36, 39, 40
V_SEL = DEPTH * VL
V_WM = V_SEL + 4
V_CI = V_WM + 16
V_IW = V_CI + 16
NV = V_IW + 1


class Buf:
    __slots__ = ("w", "r")

    def __init__(self):
        self.w = {}
        self.r = {}


class Eng:
    def __init__(self, name, sem):
        self.name, self.sem, self.cnt, self.waited, self.plan = name, sem, 0, {}, []


class Slot:
    def __init__(self, sem):
        self.sem, self.cnt = sem, 0


class Tracker:
    def __init__(self, nc, es):
        self.nc, self.es = nc, es
        self.engs = {n: Eng(n, es.enter_context(nc.semaphore("e_" + n))) for n in ("pe", "act", "dve", "pool", "sp")}
        self.slots = []
        self.slot_of = {}

    def slot(self, name):
        sl = Slot(self.es.enter_context(self.nc.semaphore("d_" + name)))
        self.slots.append(sl)
        self.slot_of[sl.sem] = sl
        return sl

    def _waits(self, e, reads, writes):
        deps = []
        for b in reads:
            deps.extend(b.w.items())
        for b in writes:
            deps.extend(b.w.items())
            deps.extend(b.r.items())
        for sem, val in deps:
            if e.name == "pe" and sem is e.sem:
                continue
            sl = self.slot_of.get(sem)
            if sl is not None:
                val = sl.cnt
            if e.waited.get(sem, 0) < val:
                e.plan.append(("w", sem, val))
                e.waited[sem] = val

    def _mark(self, dep, reads, writes):
        for b in reads:
            if b.r.get(dep[0], 0) < dep[1]:
                b.r[dep[0]] = dep[1]
        for b in writes:
            if b.w.get(dep[0], 0) < dep[1]:
                b.w[dep[0]] = dep[1]

    def op(self, en, fn, reads=(), writes=()):
        e = self.engs[en]
        self._waits(e, reads, writes)
        e.cnt += 1
        e.plan.append(("o", fn, e.sem, 1))
        self._mark((e.sem, e.cnt), reads, writes)

    def dma(self, fn, slot, reads=(), writes=(), n=1, en="sp"):
        e = self.engs[en]
        self._waits(e, reads, writes)
        slot.cnt += 16 * n
        e.plan.append(("o", fn, slot.sem, 16))
        self._mark((slot.sem, slot.cnt), reads, writes)

    def cc(self, fn, slot, reads=(), writes=()):
        e = self.engs["pool"]
        self._waits(e, reads, writes)
        slot.cnt += 1
        e.plan.append(("o", fn, slot.sem, 1))
        self._mark((slot.sem, slot.cnt), reads, writes)

    def replay(self, en, h, final=False):
        e = self.engs[en]
        for it in e.plan:
            if it[0] == "w":
                h.wait_ge(it[1], it[2])
            else:
                r = it[1](h)
                for ins in (r if isinstance(r, (list, tuple)) else [r]):
                    ins.then_inc(it[2], it[3])
        if final:
            for sl in self.slots:
                if sl.cnt:
                    h.wait_ge(sl.sem, sl.cnt)
            for o in self.engs.values():
                if o.cnt and o is not e:
                    h.wait_ge(o.sem, o.cnt)


class Rot:
    def __init__(self, tiles):
        self.tiles, self.i = tiles, 0

    def get(self):
        t = self.tiles[self.i % len(self.tiles)]
        self.i += 1
        return t


def build(depth=DEPTH, debug=False):
    nc = bass.Bass("TRN2", target_bir_lowering=False)
    es = ExitStack()
    T = Tracker(nc, es)

    def dram(name, shape, dtype=F32, kind="ExternalInput"):
        return nc.dram_tensor(name, list(shape), dtype, kind=kind).ap()

    d_xin = dram("xin", [L, 512])
    d_xs = dram("xs", [NS, 512])
    d_vecs = dram("vecs", [128, NV])
    d_mkn = dram("mkn", [128, DEPTH * 128])
    d_ident = dram("ident", [128, 128])
    d_masks = dram("masks", [128, 12 * 512])
    d_win = dram("win", [DEPTH, D, SLOTC])
    d_wout = dram("wout", [DEPTH, 1536, 512])
    d_wm = dram("wm", [DEPTH, D, 256])
    d_pw = dram("pw", [DEPTH, 128, 128])
    d_mem = dram("mem", [256, D])
    d_ck = [dram(f"ck{g}", [DEPTH, NS, WINS[g], 2, 128]) for g in range(3)]
    d_cm = dram("cm", [DEPTH, NS, 256, 2, 128])
    d_sp = dram("spool", [DEPTH, NS, 15, 128])
    EO = "ExternalOutput"
    o_yp = dram("o_yp", [L, 512], kind=EO)
    o_ys = dram("o_ys", [NS, 512], kind=EO)
    o_spp = dram("o_spp", [DEPTH, 15, 128], kind=EO)
    o_cp = [dram(f"o_cp{g}", [DEPTH, WINS[g], 2, 128], kind=EO) for g in range(3)]
    o_cmp = dram("o_cmp", [DEPTH, 256, 2, 128], kind=EO)
    o_sps = dram("o_sps", [DEPTH, NS, 15, 128], kind=EO)
    o_cs = [dram(f"o_cs{g}", [DEPTH, NS, WINS[g], 2, 128], kind=EO) for g in range(3)]
    if debug:
        dbg_od = dram("dbg_od", [128, 512], kind=EO)
        dbg_rz = dram("dbg_rz", [128, 512], kind=EO)
        dbg_y = dram("dbg_y", [128, 3, 512], BF16, kind=EO)
        dbg_q = dram("dbg_q", [128, 3, 512], BF16, kind=EO)
    IN = "Internal"
    xres_d = dram("xres_d", [512, L + SW], kind=IN)
    xg_in = [[dram(f"xgi{l}_{st}", [512, 1024], BF16, IN) for st in range(4)] + [dram(f"xgi{l}_s", [512, SW], BF16, IN)] for l in range(depth)]
    xg_out = [[dram(f"xgo{l}_{st}", [2048, 1024], BF16, IN) for st in range(4)] + [dram(f"xgo{l}_s", [2048, SW], BF16, IN)] for l in range(depth)]
    yg_in = [[dram(f"ygi{l}_{st}", [384, 1024], BF16, IN) for st in range(4)] + [dram(f"ygi{l}_s", [384, SW], BF16, IN)] for l in range(depth)]
    yg_out = [[dram(f"ygo{l}_{st}", [1536, 1024], BF16, IN) for st in range(4)] + [dram(f"ygo{l}_s", [1536, SW], BF16, IN)] for l in range(depth)]
    B_xgi = [[Buf() for _ in range(5)] for _ in range(depth)]
    B_xgo = [[Buf() for _ in range(5)] for _ in range(depth)]
    B_ygi = [[Buf() for _ in range(5)] for _ in range(depth)]
    B_ygo = [[Buf() for _ in range(5)] for _ in range(depth)]
    B_xres = [Buf() for _ in range(NTT + 1)]

    def sb(name, shape, dtype=F32):
        return es.enter_context(nc.sbuf_tensor(name, list(shape), dtype)), Buf()

    def ps(name):
        return es.enter_context(nc.psum_tensor(name, [128, 512], F32)), Buf()

    ident_f, B_idf = sb("ident_f", [128, 128])
    ident_b, B_idb = sb("ident_b", [128, 128], BF16)
    ones_b, B_ones = sb("ones_b", [128, 128], BF16)
    vecs, B_vecs = sb("vecs_t", [128, NV])
    mkn, B_mkn = sb("mkn_t", [128, 128])
    masks, B_masks = sb("masks_t", [128, 12 * 512], BF16)
    wi, B_wi = sb("wi", [128, NCH, SLOTC], BF16)
    wo, B_wo = sb("wo", [128, 12, 512], BF16)
    pw, B_pw = sb("pw_t", [128, 128], BF16)
    KmT, B_KmT = sb("KmT", [128, 256], BF16)
    Vm, B_Vm = sb("Vm", [128, 2, 128], BF16)
    KSL = (8, 8, 32)
    kT = [sb(f"kT{g}", [128, KSL[g] * 128], BF16)[0] for g in range(3)]
    B_k = [[Buf() for _ in range(KSL[g])] for g in range(3)]
    vT2, B_vT2 = sb("vT2", [128, L], BF16)
    Vb = [sb(f"Vb{g}", [128, KSL[g], 128], BF16)[0] for g in range(3)]
    B_v = [[Buf() for _ in range(KSL[g])] for g in range(3)]
    HT0 = sb("hT0", [128, NCH, TT], BF16)
    HT = [HT0, HT0]
    stage = sb("stage", [128, SLOTC // 2])
    ubuf, B_u = sb("ubuf", [128, 16 + TT])
    rstdx, B_rx = sb("rstdx", [128, TT])
    gates = [sb(f"gate{i}", [128, TT], BF16) for i in range(3)]
    QD = [sb(f"qd{g}", [128, TT], BF16) for g in range(3)]
    QM = sb("qm", [128, TT], BF16)
    VD = Rot([sb(f"vd{i}", [128, TT], BF16) for i in range(1)])
    SQ = Rot([sb(f"sq{i}", [128, TT], BF16) for i in range(2)])
    ZT = Rot([sb(f"zt{i}", [128, TT]) for i in range(2)])
    NF = Rot([sb(f"nf{i}", [128, TT]) for i in range(2)])
    RQ = Rot([sb(f"rq{i}", [128, TT]) for i in range(1)])
    PT = Rot([sb(f"pt{i}", [128, TT], BF16) for i in range(2)])
    YB = Rot([sb(f"yb{i}", [128, 3, TT], BF16) for i in range(1)])
    PA = sb("pa", [128, 16 + TT])
    PB = sb("pb", [128, 16 + TT])
    PACC = sb("pacc", [128, TT])
    DTb = sb("dtb", [128, TT], BF16)
    CST = Rot([sb(f"cst{i}", [128, 512]) for i in range(2)])
    YT, B_YT = sb("yT", [128, 12, TT], BF16)
    XR, B_XR = sb("xr", [128, 4, TT])
    XB, B_XB = sb("xb", [128, 4, TT], BF16)
    wmk, B_wmk = YT[:].rearrange("p k t -> p (k t)")[:, 0:NCH * 256].rearrange("p (c n) -> p c n", c=NCH), B_YT
    memT, B_memT = HT0[0][:].rearrange("p c t -> p (c t)")[:, 0:NCH * 256].rearrange("p (c n) -> p c n", c=NCH), HT0[1]
    memt, B_memt = XR[:].rearrange("p c t -> p (c t)"), B_XR
    UE, B_UE = sb("ue", [128, NS, 16])
    UP, B_UP = sb("up", [128, NS, 16])
    USUM, B_US = sb("usum", [128, NS])
    KSf = [sb(f"ksf{g}", [128, SW]) for g in range(3)]
    KSb = [sb(f"ksb{g}", [128, SW], BF16) for g in range(3)]
    VSf = [sb(f"vsf{g}", [128, SW]) for g in range(3)]
    KC = Rot([sb(f"kc{i}", [128, 2, 2, 128]) for i in range(1)])
    KCT = Rot([sb(f"kct{i}", [128, 256], BF16) for i in range(2)])
    VCb = Rot([sb(f"vcb{i}", [128, 2, 128], BF16) for i in range(2)])
    PSs = Rot([sb(f"pss{i}", [128, 4], BF16) for i in range(4)])
    PSB, B_PSB = sb("psb", [128, 16])
    OSF, B_OSF = sb("osf", [128, SW])
    TMP4, B_TMP4 = sb("tmp4", [128, SW])
    small = Rot([sb(f"sm{i}", [128, 8]) for i in range(4)])
    ZP = Rot([ps("zp0"), ps("zp1")])
    SP = Rot([ps("sp0"), ps("sp1")])
    OP, B_OP = ps("op")
    ZZ, B_ZZ = ps("zz")
    TP, B_TP = ps("tp")
    NQ, B_NQ = ps("nq")
    TPb = TP[:].bitcast(BF16)

    sl_misc = T.slot("misc")
    sl_h = [T.slot("h0"), T.slot("h1")]
    sl_w = T.slot("w")
    sl_out = T.slot("out")
    sl_x = T.slot("x")
    sl_y = T.slot("y")
    sl_cc = T.slot("ccx")
    sl_ccy = T.slot("ccy")
    sl_bulk = T.slot("bulk")
    sl_kc = T.slot("kc")

    def vcol(c, n=1):
        return vecs[:, c:c + n]

    def ld(dst, src, bw, slot=sl_misc, reads=(), **kw):
        T.dma(lambda e: e.dma_start(out=dst, in_=src, **kw), slot, reads=reads, writes=bw)

    ld(vecs[:], d_vecs[:, :], [B_vecs])
    ld(ident_f[:], d_ident[:, :], [B_idf])
    T.op("dve", lambda e: e.tensor_copy(out=ident_b[:], in_=ident_f[:]), [B_idf], [B_idb])
    T.op("dve", lambda e: e.memset(ones_b[:], 1.0), [], [B_ones])
    for i in range(12):
        st_t, st_b = stage
        ld(st_t[:, 0:512], d_masks[:, i * 512:(i + 1) * 512], [st_b])
        T.op("dve", lambda e, i=i: e.tensor_copy(out=masks[:, i * 512:(i + 1) * 512], in_=stage[0][:, 0:512]), [st_b], [B_masks])
    allk = [b for g in range(3) for b in B_k[g]]
    allv = [b for g in range(3) for b in B_v[g]]
    for g in range(3):
        T.op("dve", lambda e, g=g: e.memset(kT[g][:], 0.0), [], B_k[g])
        T.op("dve", lambda e, g=g: e.memset(Vb[g][:].rearrange("p a b -> p (a b)"), 0.0), [], B_v[g])
    T.op("dve", lambda e: e.memset(vT2[:], 0.0), [], [B_vT2])
    T.op("dve", lambda e: e.memset(ubuf[:], 0.0), [], [B_u])

    def transpose_f32(src_ap, cols, dst_col):
        return lambda e: e.transpose(out=TP[0:cols, dst_col:dst_col + 128], in_=src_ap, identity=ident_f[:])

    xres_v = xres_d.rearrange("(c p) t -> p c t", p=128)

    def x_init_tile(tt, w, src_rows):
        nb = 4 if w == TT else 1
        for j in range(nb):
            ct, cb = CST.get()
            if w == TT:
                ld(ct[:], d_xin[tt * TT + j * 128: tt * TT + (j + 1) * 128, :], [cb], slot=sl_x)
            else:
                T.op("dve", lambda e, ct=ct: e.memset(ct[:], 0.0), [], [cb])
                ld(ct[0:NS, :], d_xs[:, :], [cb], slot=sl_x)

            def f(e, ct=ct):
                r = None
                for c in range(4):
                    r = e.transpose(out=TP[:, c * 128:(c + 1) * 128], in_=ct[:, c * 128:(c + 1) * 128], identity=ident_f[:])
                return r
            T.op("pe", f, [cb, B_idf], [B_TP])
            cw = 128 if w == TT else SW
            T.op("dve", lambda e, j=j, cw=cw: e.tensor_copy(out=XR[:, :, j * 128:j * 128 + cw], in_=TP[:].rearrange("p (c t) -> p c t", c=4)[:, :, 0:cw]), [B_TP], [B_XR])
        T.op("act", lambda e: e.copy(out=XB[:, :, 0:w], in_=XR[:, :, 0:w]), [B_XR], [B_XB])
        off = tt * TT if w == TT else L
        T.dma(lambda e: e.dma_start(out=xres_v[:, :, off:off + w], in_=XR[:, :, 0:w]), sl_x, reads=[B_XR], writes=[B_xres[tt]])
        st = tt // 2 if w == TT else 4
        o2 = (tt % 2) * TT if w == TT else 0
        T.dma(lambda e: e.dma_start(out=xg_in[0][st].rearrange("(c p) t -> p c t", p=128)[:, :, o2:o2 + w], in_=XB[:, :, 0:w]), sl_x,
              reads=[B_XB], writes=[B_xgi[0][st]])

    def ag(kind_in, kind_out, Bi, Bo, slot=None):
        T.cc(lambda e: e.collective_compute("AllGather", ALU.bypass, replica_groups=GROUPS, ins=[kind_in], outs=[kind_out]),
             slot or sl_cc, reads=[Bi], writes=[Bo])

    x_init_tile(0, TT, None)
    for tt in range(NTT):
        x_init_tile(tt, TT, None)
        if tt % 2 == 1:
            ag(xg_in[0][tt // 2], xg_out[0][tt // 2], B_xgi[0][tt // 2], B_xgo[0][tt // 2])
    x_init_tile(NTT, SW, None)
    ag(xg_in[0][4], xg_out[0][4], B_xgi[0][4], B_xgo[0][4])

    for g in range(3):
        W = WINS[g]
        for l in range(depth):
            for i in range(NS):
                T.dma(lambda e, g=g, l=l, i=i, W=W: e.dma_start(
                    out=o_cs[g][l, i, 0:W - 1, :, :].rearrange("w j d -> (w j d)").rearrange("(a b) -> a b", b=256),
                    in_=d_ck[g][l, i, 1:W, :, :].rearrange("w j d -> (w j d)").rearrange("(a b) -> a b", b=256)), sl_bulk)

    def load_w_in(l):
        vb = l * VL
        st_t, st_b = stage
        HW = SLOTC // 2
        for c in range(NCH):
            for hf in range(2):
                ld(st_t[:], d_win[l, c * 128:(c + 1) * 128, hf * HW:(hf + 1) * HW], [st_b], slot=sl_w)
                T.op("dve", lambda e, c=c, hf=hf: e.tensor_scalar(out=wi[:, c, hf * HW:(hf + 1) * HW], in0=stage[0][:], scalar1=vcol(vb + V_NG + c), scalar2=None, op0=ALU.mult),
                     [st_b, B_vecs], [B_wi])
        for c in range(NCH):
            ld(st_t[:, 0:256], d_wm[l, c * 128:(c + 1) * 128, :], [st_b], slot=sl_w)
            T.op("dve", lambda e, c=c: e.tensor_scalar(out=wmk[:, c, :], in0=stage[0][:, 0:256], scalar1=vcol(vb + V_MG + c), scalar2=None, op0=ALU.mult),
                 [st_b, B_vecs], [B_wmk])
        ld(st_t[:, 0:128], d_pw[l, :, :], [st_b], slot=sl_w)
        T.op("dve", lambda e: e.tensor_copy(out=pw[:], in_=stage[0][:, 0:128]), [st_b], [B_pw])

    def load_w_out(l):
        st_t, st_b = stage
        for k in range(12):
            ld(st_t[:, 0:512], d_wout[l, k * 128:(k + 1) * 128, :], [st_b], slot=sl_w)
            T.op("dve", lambda e, k=k: e.tensor_copy(out=wo[:, k, :], in_=stage[0][:, 0:512]), [st_b], [B_wo])

    def mem_kv(l):
        ld(mkn[:], d_mkn[:, l * 128:(l + 1) * 128], [B_mkn])
        for mb in range(2):
            ld(memt[:], d_mem[mb * 128:(mb + 1) * 128, :], [B_memt])
            sm, smb = small.get()
            for q in range(4):
                sq, sqb = SQ.get()
                T.op("act", lambda e, sm=sm, sq=sq, q=q: e.activation(out=sq[:, 0:512], in_=memt[:, q * 512:(q + 1) * 512], func=AF.Square, accum_out=sm[:, q:q + 1]), [B_memt], [sqb, smb])
            T.op("dve", lambda e, sm=sm: e.reduce_sum(out=sm[:, 4:5], in_=sm[:, 0:4], axis=AX.X), [smb], [smb])
            T.op("act", lambda e, sm=sm: e.activation(out=sm[:, 5:6], in_=sm[:, 4:5], func=AF.Ln, scale=1.0 / D, bias=EPS), [smb], [smb])
            T.op("act", lambda e, sm=sm: e.activation(out=sm[:, 6:7], in_=sm[:, 5:6], func=AF.Exp, scale=-0.5), [smb], [smb])
            T.op("dve", lambda e, sm=sm: e.tensor_scalar(out=memt[:], in0=memt[:], scalar1=sm[:, 6:7], scalar2=None, op0=ALU.mult), [smb, B_memt], [B_memt])
            for q in range(4):
                def f(e, q=q):
                    r = None
                    for j in range(4):
                        c = q * 4 + j
                        r = e.transpose(out=TP[:, j * 128:(j + 1) * 128], in_=memt[:, c * 128:(c + 1) * 128], identity=ident_f[:])
                    return r
                T.op("pe", f, [B_memt, B_idf], [B_TP])
                T.op("dve", lambda e, q=q, mb=mb: e.tensor_copy(out=memT[:, q * 4:(q + 1) * 4, mb * 128:(mb + 1) * 128],
                                                             in_=TP[:].rearrange("p (j t) -> p j t", j=4)), [B_TP], [B_memT])
        for mb in range(2):
            zp, zb = ZP.get()

            def f(e, mb=mb, zp=zp):
                r = None
                for c in range(NCH):
                    r = e.matmul(zp[:, 0:256], lhsT=memT[:, c, mb * 128:(mb + 1) * 128], rhs=wmk[:, c, :], start=(c == 0), stop=(c == NCH - 1))
                return r
            T.op("pe", f, [B_memT, B_wmk], [zb])
            ct, cb = CST.get()
            sm, smb = small.get()
            nf, nfb = NF.get()
            T.op("act", lambda e, zp=zp, nf=nf, sm=sm: e.activation(out=nf[:, 0:128], in_=zp[:, 0:128], func=AF.Square, accum_out=sm[:, 0:1]), [zb], [nfb, smb])
            T.op("act", lambda e, sm=sm: e.activation(out=sm[:, 1:2], in_=sm[:, 0:1], func=AF.Ln, scale=1.0 / 128, bias=EPS), [smb], [smb])
            T.op("act", lambda e, sm=sm: e.activation(out=sm[:, 2:3], in_=sm[:, 1:2], func=AF.Exp, scale=-0.5), [smb], [smb])
            T.op("dve", lambda e, zp=zp, ct=ct, sm=sm: e.scalar_tensor_tensor(out=ct[:, 0:128], in0=zp[:, 0:128], scalar=sm[:, 2:3], in1=mkn[:, :],
                                                                           op0=ALU.mult, op1=ALU.mult), [zb, smb, B_mkn], [cb])
            T.op("act", lambda e, zp=zp, ct=ct: e.copy(out=ct[:, 128:256], in_=zp[:, 128:256]), [zb], [cb])
            T.dma(lambda e, ct=ct, mb=mb: e.dma_start(out=o_cmp[l, mb * 128:(mb + 1) * 128, :, :], in_=ct[:, 0:256].rearrange("p (j d) -> p j d", j=2)), sl_out, reads=[cb])
            T.op("dve", lambda e, ct=ct, mb=mb: e.tensor_copy(out=Vm[:, mb, :], in_=ct[:, 128:256]), [cb], [B_Vm])
            T.op("pe", lambda e, ct=ct: e.transpose(out=TP[:, 0:128], in_=ct[:, 0:128], identity=ident_f[:]), [cb, B_idf], [B_TP])
            T.op("dve", lambda e, mb=mb: e.tensor_copy(out=KmT[:, mb * 128:(mb + 1) * 128], in_=TP[:, 0:128]), [B_TP], [B_KmT])

    def norm_rstd(src, srcb, w, n, from_psum_ssq=None):
        sq, sqb = SQ.get()
        T.op("pool", lambda e: e.tensor_tensor(out=sq[:, 0:w], in0=src, in1=src, op=ALU.mult), [srcb], [sqb])
        T.op("pe", lambda e: e.matmul(NQ[:, 0:w], lhsT=ones_b[:], rhs=sq[:, 0:w], start=True, stop=True), [sqb, B_ones], [B_NQ])
        rq, rqb = RQ.get()
        T.op("act", lambda e: e.activation(out=rq[:, 0:w], in_=NQ[:, 0:w], func=AF.Ln, scale=1.0 / n, bias=EPS), [B_NQ], [rqb])
        T.op("act", lambda e: e.activation(out=rq[:, 0:w], in_=rq[:, 0:w], func=AF.Exp, scale=-0.5), [rqb], [rqb])
        return rq, rqb

    def out_tokmajor(src, srcb, w, dst_fn):
        nb = (w + 127) // 128
        for j in range(nb):
            cw = min(128, w - j * 128)
            T.op("pe", lambda e, j=j, cw=cw: e.transpose(out=TP[0:cw, 0:128], in_=src[:, j * 128:j * 128 + cw], identity=ident_f[:]), [srcb, B_idf], [B_TP])
            ct, cb = CST.get()
            T.op("act", lambda e, ct=ct, cw=cw: e.copy(out=ct[0:cw, 0:128], in_=TP[0:cw, 0:128]), [B_TP], [cb])
            dst = dst_fn(j, cw)
            T.dma(lambda e, ct=ct, cw=cw, dst=dst: e.dma_start(out=dst, in_=ct[0:cw, 0:128]), sl_out, reads=[cb])

    def deint(ap, dil):
        return ap.rearrange("p (r a) -> p a r", r=dil)

    def nat(ap, dil):
        return ap.rearrange("p (a r) -> p a r", r=dil)

    def phase1(l, tt):
        sample = (tt == NTT)
        w = SW if sample else TT
        st = 4 if sample else tt // 2
        off = 0 if sample else (tt % 2) * TT
        vb = l * VL
        hbuf, hb = HT[tt % 2]
        T.dma(lambda e: e.dma_start(out=hbuf[:, :, 0:w], in_=xg_out[l][st].rearrange("(c p) t -> p c t", p=128)[:, :, off:off + w]),
              sl_h[tt % 2], reads=[B_xgo[l][st]], writes=[hb])
        for c in range(NCH):
            sq, sqb = SQ.get()
            T.op("pool", lambda e, c=c, sq=sq: e.tensor_tensor(out=sq[:, 0:w], in0=hbuf[:, c, 0:w], in1=hbuf[:, c, 0:w], op=ALU.mult), [hb], [sqb])
            T.op("pe", lambda e, c=c, sq=sq: e.matmul(NQ[:, 0:w], lhsT=ones_b[:], rhs=sq[:, 0:w], start=(c == 0), stop=(c == NCH - 1), skip_group_check=True),
                 [sqb, B_ones], [B_NQ])
        T.op("act", lambda e: e.activation(out=rstdx[:, 0:w], in_=NQ[:, 0:w], func=AF.Ln, scale=1.0 / D, bias=EPS), [B_NQ], [B_rx])
        T.op("act", lambda e: e.activation(out=rstdx[:, 0:w], in_=rstdx[:, 0:w], func=AF.Exp, scale=-0.5), [B_rx], [B_rx])

        def zchunk(j):
            zp, zb = ZP.get()

            def f(e):
                r = None
                for c in range(NCH):
                    r = e.matmul(zp[:, 0:w], lhsT=wi[:, c, j * 128:(j + 1) * 128], rhs=hbuf[:, c, 0:w], start=(c == 0), stop=(c == NCH - 1))
                return r
            T.op("pe", f, [B_wi, hb], [zb])
            return zp, zb

        def evac(zp, zb, dst, dstb):
            T.op("dve", lambda e: e.tensor_tensor(out=dst, in0=zp[:, 0:w], in1=rstdx[:, 0:w], op=ALU.mult), [zb, B_rx], [dstb])

        zp, zb = zchunk(0)
        evac(zp, zb, ubuf[:, 16:16 + w], B_u)
        for gi, j in enumerate((1, 11, 13)):
            zp, zb = zchunk(j)
            zt, ztb = ZT.get()
            evac(zp, zb, zt[:, 0:w], ztb)
            gt, gb = gates[gi]
            T.op("act", lambda e, zt=zt, gt=gt: e.activation(out=gt[:, 0:w], in_=zt[:, 0:w], func=AF.Silu), [ztb], [gb])
        for g in range(3):
            dil, W = DILS[g], WINS[g]
            zp, zb = zchunk(2 + 3 * g)
            zt, ztb = ZT.get()
            evac(zp, zb, zt[:, 0:w], ztb)
            rq, rqb = norm_rstd(zt[:, 0:w], ztb, w, 128)
            qd, qdb = QD[g]
            if sample:
                T.op("dve", lambda e, zt=zt, rq=rq, qd=qd, g=g: e.scalar_tensor_tensor(out=qd[:, 0:w], in0=zt[:, 0:w], scalar=vcol(vb + V_QN + g), in1=rq[:, 0:w],
                                                                                 op0=ALU.mult, op1=ALU.mult), [ztb, rqb, B_vecs], [qdb])
            else:
                T.op("dve", lambda e, zt=zt, rq=rq, qd=qd, g=g, dil=dil: e.scalar_tensor_tensor(out=deint(qd[:, 0:w], dil), in0=nat(zt[:, 0:w], dil), scalar=vcol(vb + V_QN + g),
                                                                                          in1=nat(rq[:, 0:w], dil), op0=ALU.mult, op1=ALU.mult), [ztb, rqb, B_vecs], [qdb])
            zp, zb = zchunk(3 + 3 * g)
            zt, ztb = ZT.get()
            evac(zp, zb, zt[:, 0:w], ztb)
            rq, rqb = norm_rstd(zt[:, 0:w], ztb, w, 128)
            if sample:
                kf, kfb = KSf[g]
                T.op("dve", lambda e, zt=zt, rq=rq, kf=kf, g=g: e.scalar_tensor_tensor(out=kf[:, 0:w], in0=zt[:, 0:w], scalar=vcol(vb + V_KN + g), in1=rq[:, 0:w],
                                                                                 op0=ALU.mult, op1=ALU.mult), [ztb, rqb, B_vecs], [kfb])
                T.op("act", lambda e, kf=kf, g=g: e.copy(out=KSb[g][0][:, 0:w], in_=kf[:, 0:w]), [kfb], [KSb[g][1]])
                T.dma(lambda e, kf=kf, g=g, W=W: e.dma_start(out=o_cs[g][l, :, W - 1, 0, :].rearrange("i d -> d i"), in_=kf[:, 0:NS], allow_slow_non_contiguous=True),
                      sl_out, reads=[kfb])
            else:
                nf, nfb = NF.get()
                T.op("dve", lambda e, zt=zt, rq=rq, nf=nf, g=g: e.scalar_tensor_tensor(out=nf[:, 0:w], in0=zt[:, 0:w], scalar=vcol(vb + V_KN + g), in1=rq[:, 0:w],
                                                                                 op0=ALU.mult, op1=ALU.mult), [ztb, rqb, B_vecs], [nfb])
                na = TT // dil
                i0 = (tt % 2) * na if g < 2 else tt * na
                kview = kT[g][:].rearrange("p (r i) -> p i r", r=dil)[:, i0:i0 + na, :]
                kblocks = blocks_of(g, tt)
                T.op("act", lambda e, nf=nf, kview=kview, dil=dil: e.copy(out=kview, in_=nat(nf[:, 0:w], dil)), [nfb], [B_k[g][b] for b in kblocks])
                t0 = tt * TT
                if t0 + TT > L - W:
                    lo = max(t0, L - W)

                    def dstk(j, cw, g=g, W=W, t0=t0):
                        r0 = t0 + j * 128 - (L - W)
                        return o_cp[g][l, r0:r0 + cw, 0, :]
                    if lo == t0:
                        out_tokmajor(nf[:, 0:w], nfb, w, dstk)
                    else:
                        out_tokmajor(nf[:, lo - t0:w], nfb, w - (lo - t0), lambda j, cw, g=g, W=W, lo=lo: o_cp[g][l, lo + j * 128 - (L - W): lo + j * 128 - (L - W) + cw, 0, :])
            zp, zb = zchunk(4 + 3 * g)
            if sample:
                vf, vfb = VSf[g]
                evac(zp, zb, vf[:, 0:w], vfb)
                T.dma(lambda e, vf=vf, g=g, W=W: e.dma_start(out=o_cs[g][l, :, W - 1, 1, :].rearrange("i d -> d i"), in_=vf[:, 0:NS], allow_slow_non_contiguous=True),
                      sl_out, reads=[vfb])
            else:
                nf, nfb = NF.get()
                evac(zp, zb, nf[:, 0:w], nfb)
                na = TT // dil
                vblocks = blocks_of(g, tt)
                if g < 2:
                    vd, vdb = VD.get()
                    T.op("act", lambda e, nf=nf, vd=vd, dil=dil: e.copy(out=deint(vd[:, 0:w], dil), in_=nat(nf[:, 0:w], dil)), [nfb], [vdb])

                    def f(e, vd=vd):
                        r = None
                        for i in range(4):
                            r = e.transpose(out=TPb[:, i * 128:(i + 1) * 128], in_=vd[:, i * 128:(i + 1) * 128], identity=ident_b[:])
                        return r
                    T.op("pe", f, [vdb, B_idb], [B_TP])
                    b0 = vblocks[0]
                    if g == 0:
                        T.op("dve", lambda e, b0=b0, g=g: e.tensor_copy(out=Vb[g][:, b0:b0 + 4, :], in_=TPb[:, 0:512].rearrange("p (i d) -> p i d", i=4)), [B_TP], [B_v[g][b] for b in vblocks])
                    else:
                        T.op("dve", lambda e, g=g: e.tensor_copy(out=Vb[g][:].rearrange("p (r m) d -> p r m d", r=4)[:, :, tt % 2, :],
                                                              in_=TPb[:, 0:512].rearrange("p (i d) -> p i d", i=4)), [B_TP], [B_v[g][b] for b in vblocks])
                else:
                    i0 = tt * na
                    vview = vT2[:].rearrange("p (r i) -> p i r", r=dil)[:, i0:i0 + na, :]
                    T.op("act", lambda e, nf=nf, vview=vview, dil=dil: e.copy(out=vview, in_=nat(nf[:, 0:w], dil)), [nfb], [B_vT2])
                    m = tt // 4
                    for q in range(4):
                        def f(e, q=q, m=m):
                            r = None
                            for i in range(4):
                                rr = q * 4 + i
                                r = e.transpose(out=TPb[:, i * 128:(i + 1) * 128], in_=vT2[:, rr * 256 + m * 128: rr * 256 + (m + 1) * 128], identity=ident_b[:])
                            return r
                        T.op("pe", f, [B_vT2, B_idb], [B_TP])
                        T.op("dve", lambda e, q=q, m=m: e.tensor_copy(out=Vb[2][:].rearrange("p (r m) d -> p r m d", r=16)[:, q * 4:(q + 1) * 4, m, :],
                                                                   in_=TPb[:, 0:512].rearrange("p (i d) -> p i d", i=4)), [B_TP], [B_v[2][(q * 4 + i) * 2 + m] for i in range(4)])
                t0 = tt * TT
                if t0 + TT > L - W:
                    lo = max(t0, L - W)
                    out_tokmajor(nf[:, lo - t0:w], nfb, w - (lo - t0), lambda j, cw, g=g, W=W, lo=lo: o_cp[g][l, lo + j * 128 - (L - W): lo + j * 128 - (L - W) + cw, 1, :])
        zp, zb = zchunk(12)
        zt, ztb = ZT.get()
        evac(zp, zb, zt[:, 0:w], ztb)
        rq, rqb = norm_rstd(zt[:, 0:w], ztb, w, 128)
        qm, qmb = QM
        T.op("dve", lambda e, zt=zt, rq=rq: e.scalar_tensor_tensor(out=qm[:, 0:w], in0=zt[:, 0:w], scalar=vcol(vb + V_MQ), in1=rq[:, 0:w], op0=ALU.mult, op1=ALU.mult),
             [ztb, rqb, B_vecs], [qmb])
        yb, ybb = YB.get()
        if sample:
            sample_attn(l, yb, ybb)
        else:
            prompt_attn(l, tt, yb, ybb)
        if debug and l == 0 and tt == 0:
            T.dma(lambda e: e.dma_start(out=dbg_y[:, :, :], in_=yb[:, :, :]), sl_out, reads=[ybb])
        T.dma(lambda e: e.dma_start(out=yg_in[l][st].rearrange("(j p) t -> p j t", p=128)[:, :, off:off + w], in_=yb[:, :, 0:w]), sl_y, reads=[ybb], writes=[B_ygi[l][st]])
        if sample or tt % 2 == 1:
            ag(yg_in[l][st], yg_out[l][st], B_ygi[l][st], B_ygo[l][st], sl_ccy)

    def blocks_of(g, tt):
        if g == 0:
            return [(4 * tt + i) % 8 for i in range(4)]
        if g == 1:
            return [r * 2 + tt % 2 for r in range(4)]
        return [r * 2 + tt // 4 for r in range(16)]

    def kblock_ap(g, blk):
        return kT[g][:, blk * 128:(blk + 1) * 128]

    def finish_attn(w, gate, gateb, yb, ybb, yj, dbg=False):
        rq, rqb = RQ.get()
        T.op("act", lambda e: e.activation(out=rq[:, 0:w], in_=ZZ[:, 0:w], func=AF.Ln), [B_ZZ], [rqb])
        T.op("act", lambda e: e.activation(out=rq[:, 0:w], in_=rq[:, 0:w], func=AF.Exp, scale=-1.0), [rqb], [rqb])
        nf, nfb = NF.get()
        T.op("dve", lambda e: e.tensor_tensor(out=nf[:, 0:w], in0=OP[:, 0:w], in1=rq[:, 0:w], op=ALU.mult), [B_OP, rqb], [nfb])
        T.op("dve", lambda e: e.tensor_tensor(out=yb[:, yj, 0:w], in0=nf[:, 0:w], in1=gate[:, 0:w], op=ALU.mult), [nfb, gateb], [ybb])
        if dbg:
            T.dma(lambda e: e.dma_start(out=dbg_od[:, :], in_=nf[:, 0:512]), sl_out, reads=[nfb])
            T.dma(lambda e: e.dma_start(out=dbg_rz[:, :], in_=rq[:, 0:512]), sl_out, reads=[rqb])

    def prompt_attn(l, tt, yb, ybb):
        vb = l * VL
        w = TT
        first = [True]
        for g in range(3):
            dil = DILS[g]
            nseg = 4 if g < 2 else 16
            sw_ = TT // nseg
            for typ in (0, 1):
                segs = []
                nvalid = 0
                for i in range(nseg):
                    if g == 0:
                        kb = 4 * tt + i - (1 - typ)
                        ok = kb >= 0
                        kb = kb % 8
                    elif g == 1:
                        m = tt - (1 - typ)
                        kb = i * 2 + m % 2
                        ok = m >= 0
                    else:
                        m = tt // 4 - (1 - typ)
                        kb = i * 2 + m % 2
                        ok = m >= 0
                    nvalid += ok
                    segs.append((i, kb))
                c_lo, c_hi = segs[0][0] * sw_, (segs[-1][0] + 1) * sw_
                sp_, spb = SP.get()
                qd, qdb = QD[g]

                def fs(e, segs=segs, sp_=sp_, qd=qd, g=g, sw_=sw_):
                    r = None
                    for (i, kb) in segs:
                        r = e.matmul(sp_[:, i * sw_:(i + 1) * sw_], lhsT=kblock_ap(g, kb), rhs=qd[:, i * sw_:(i + 1) * sw_], start=True, stop=True, skip_group_check=True)
                    return r
                T.op("pe", fs, [qdb] + [B_k[g][kb] for (_, kb) in segs], [spb])
                pt, ptb = PT.get()
                T.op("act", lambda e, sp_=sp_, pt=pt, c_lo=c_lo, c_hi=c_hi: e.activation(out=pt[:, c_lo:c_hi], in_=sp_[:, c_lo:c_hi], func=AF.Exp, scale=SCALE), [spb], [ptb])
                if g < 2:
                    mi = typ
                else:
                    mi = 2 + (tt % 4) * 2 + typ
                if nvalid == 0:
                    mi = 10
                elif nvalid < nseg:
                    mi = 11
                T.op("pool", lambda e, pt=pt, mi=mi, c_lo=c_lo, c_hi=c_hi: e.tensor_tensor(out=pt[:, c_lo:c_hi], in0=pt[:, c_lo:c_hi], in1=masks[:, mi * 512 + c_lo: mi * 512 + c_hi], op=ALU.mult),
                     [ptb, B_masks], [ptb])

                def fo(e, segs=segs, pt=pt, g=g, sw_=sw_, dil=dil, fst=first[0]):
                    r = None
                    k = 0
                    for (i, kb) in segs:
                        if g == 0:
                            oo, zz = OP[:, i * 128:(i + 1) * 128], ZZ[:, i * 128:(i + 1) * 128]
                        else:
                            oo, zz = OP[:, i:TT:dil], ZZ[:, i:TT:dil]
                        e.matmul(oo, lhsT=Vb[g][:, kb, :], rhs=pt[:, i * sw_:(i + 1) * sw_], start=(fst and k == 0), stop=False, skip_group_check=True)
                        r = e.matmul(zz, lhsT=ones_b[:], rhs=pt[:, i * sw_:(i + 1) * sw_], start=(fst and k == 0), stop=False, skip_group_check=True)
                        k += 1
                    return r
                T.op("pe", fo, [ptb, B_ones] + [B_v[g][kb] for (_, kb) in segs], [B_OP, B_ZZ])
                first[0] = False
        finish_attn(w, gates[1][0], gates[1][1], yb, ybb, 1, dbg=(debug and l == 0 and tt == 0))
        if debug and l == 0 and tt == 0:
            for g in range(3):
                T.dma(lambda e, g=g: e.dma_start(out=dbg_q[:, g, :], in_=QD[g][0][:, 0:512]), sl_out, reads=[QD[g][1]])
        qm, qmb = QM
        for kb in range(2):
            sp_, spb = SP.get()
            T.op("pe", lambda e, sp_=sp_, kb=kb: e.matmul(sp_[:, 0:w], lhsT=KmT[:, kb * 128:(kb + 1) * 128], rhs=qm[:, 0:w], start=True, stop=True), [B_KmT, qmb], [spb])
            pt, ptb = PT.get()
            T.op("act", lambda e, sp_=sp_, pt=pt: e.activation(out=pt[:, 0:w], in_=sp_[:, 0:w], func=AF.Exp, scale=SCALE), [spb], [ptb])

            def fo(e, pt=pt, kb=kb):
                e.matmul(OP[:, 0:w], lhsT=Vm[:, kb, :], rhs=pt[:, 0:w], start=(kb == 0), stop=(kb == 1), skip_group_check=True)
                return e.matmul(ZZ[:, 0:w], lhsT=ones_b[:], rhs=pt[:, 0:w], start=(kb == 0), stop=(kb == 1), skip_group_check=True)
            T.op("pe", fo, [ptb, B_ones, B_Vm], [B_OP, B_ZZ])
        finish_attn(w, gates[2][0], gates[2][1], yb, ybb, 2)
        pa, pab = PA
        pb_, pbb = PB
        pacc, paccb = PACC
        chain = [(ubuf, B_u, pa, pab, 1, 2), (pa, pab, pb_, pbb, 2, 4), (pb_, pbb, pa, pab, 4, 8), (pa, pab, pb_, pbb, 8, 16)]
        for k, (src, srcb, dst, dstb, sh, lo) in enumerate(chain):
            T.op("pool", lambda e, src=src, dst=dst, sh=sh, lo=lo: e.tensor_tensor(out=dst[:, lo:16 + TT], in0=src[:, lo:16 + TT], in1=src[:, lo - sh:16 + TT - sh], op=ALU.add), [srcb], [dstb])
            if k == 0:
                T.op("dve", lambda e, dst=dst: e.tensor_scalar(out=pacc[:], in0=dst[:, 16:16 + TT], scalar1=vcol(V_SEL + 0), scalar2=None, op0=ALU.mult), [dstb, B_vecs], [paccb])
            else:
                T.op("dve", lambda e, dst=dst, k=k: e.scalar_tensor_tensor(out=pacc[:], in0=dst[:, 16:16 + TT], scalar=vcol(V_SEL + k), in1=pacc[:], op0=ALU.mult, op1=ALU.add),
                     [dstb, B_vecs, paccb], [paccb])
        dtb, dtbb = DTb
        if tt == 0:
            nf, nfb = NF.get()
            T.op("dve", lambda e, nf=nf: e.tensor_scalar(out=nf[:, 16:TT], in0=pacc[:, 16:TT], scalar1=vcol(V_IW), scalar2=None, op0=ALU.mult), [paccb, B_vecs], [nfb])
            T.op("dve", lambda e, nf=nf: e.tensor_tensor(out=nf[:, 0:16], in0=pacc[:, 0:16], in1=vcol(V_CI, 16), op=ALU.mult), [paccb, B_vecs], [nfb])
            T.op("dve", lambda e, nf=nf: e.tensor_tensor(out=dtb[:], in0=nf[:], in1=ubuf[:, 16:16 + TT], op=ALU.subtract), [nfb, B_u], [dtbb])
        else:
            T.op("dve", lambda e: e.scalar_tensor_tensor(out=dtb[:], in0=pacc[:], scalar=vcol(V_IW), in1=ubuf[:, 16:16 + TT], op0=ALU.mult, op1=ALU.subtract),
                 [paccb, B_vecs, B_u], [dtbb])
        zp, zb = ZP.get()
        T.op("pe", lambda e, zp=zp: e.matmul(zp[:, 0:w], lhsT=pw[:], rhs=dtb[:], start=True, stop=True), [B_pw, dtbb], [zb])
        T.op("dve", lambda e, zp=zp: e.scalar_tensor_tensor(out=yb[:, 0, 0:w], in0=zp[:, 0:w], scalar=vcol(vb + V_PS), in1=gates[0][0][:, 0:w], op0=ALU.mult, op1=ALU.mult),
             [zb, B_vecs, gates[0][1]], [ybb])
        if tt == NTT - 1:
            T.dma(lambda e: e.dma_start(out=o_spp[l, :, :].rearrange("r c -> c r"), in_=ubuf[:, 16 + TT - 15:16 + TT], allow_slow_non_contiguous=True), sl_out, reads=[B_u])
        T.op("pool", lambda e: e.tensor_copy(out=ubuf[:, 0:16], in_=ubuf[:, TT:TT + 16]), [B_u], [B_u])

    def sample_attn(l, yb, ybb):
        vb = l * VL
        w = SW
        T.op("dve", lambda e: e.memset(OSF[:], 0.0), [], [B_OSF])
        first = [True]
        for i in range(NS):
            for g in range(4):
                kc, kcb = KC.get()
                kct, kctb = KCT.get()
                vcb, vcbb = VCb.get()
                nkb = 1 if g < 3 else 2
                if g < 3:
                    W, dil = WINS[g], DILS[g]
                    T.dma(lambda e, kc=kc, g=g, i=i, W=W, dil=dil: e.dma_start(out=kc[:, 0, :, :], in_=d_ck[g][l, i, 0:W:dil, :, :]), sl_kc, writes=[kcb])
                else:
                    T.dma(lambda e, kc=kc, i=i: e.dma_start(out=kc[:], in_=d_cm[l, i, :, :, :].rearrange("(kb p) j d -> p kb j d", p=128)), sl_kc, writes=[kcb])
                for kb in range(nkb):
                    T.op("pe", lambda e, kc=kc, kb=kb: e.transpose(out=TP[:, 0:128], in_=kc[:, kb, 0, :], identity=ident_f[:]), [kcb, B_idf], [B_TP])
                    T.op("dve", lambda e, kct=kct, kb=kb: e.tensor_copy(out=kct[:, kb * 128:(kb + 1) * 128], in_=TP[:, 0:128]), [B_TP], [kctb])
                    T.op("act", lambda e, kc=kc, vcb=vcb, kb=kb: e.copy(out=vcb[:, kb, :], in_=kc[:, kb, 1, :]), [kcb], [vcbb])
                q_t, q_b = QD[g] if g < 3 else QM
                sp_, spb = SP.get()

                def fs(e, sp_=sp_, kct=kct, q_t=q_t, i=i, g=g, nkb=nkb):
                    r = None
                    for kb in range(nkb):
                        r = e.matmul(sp_[:, kb:kb + 1], lhsT=kct[:, kb * 128:(kb + 1) * 128], rhs=q_t[:, i:i + 1], start=True, stop=True, skip_group_check=True)
                    if g < 3:
                        r = e.matmul(sp_[0:1, 2:3], lhsT=KSb[g][0][:, i:i + 1], rhs=q_t[:, i:i + 1], start=True, stop=True, skip_group_check=True)
                    return r
                T.op("pe", fs, [kctb, q_b] + ([KSb[g][1]] if g < 3 else []), [spb])
                pss, pssb = PSs.get()
                T.op("act", lambda e, sp_=sp_, pss=pss, nkb=nkb: e.activation(out=pss[:, 0:nkb], in_=sp_[:, 0:nkb], func=AF.Exp, scale=SCALE), [spb], [pssb])
                if g < 3:
                    T.op("act", lambda e, sp_=sp_, pss=pss: e.activation(out=pss[0:1, 2:3], in_=sp_[0:1, 2:3], func=AF.Exp, scale=SCALE), [spb], [pssb])
                col = i if g < 3 else 8 + i
                st_flag = first[0]

                def fo(e, pss=pss, vcb=vcb, col=col, g=g, i=i, nkb=nkb, st_flag=st_flag):
                    r = None
                    for kb in range(nkb):
                        e.matmul(OP[:, col:col + 1], lhsT=vcb[:, kb, :], rhs=pss[:, kb:kb + 1], start=(st_flag and kb == 0), stop=False, skip_group_check=True)
                        r = e.matmul(ZZ[:, col:col + 1], lhsT=ones_b[:], rhs=pss[:, kb:kb + 1], start=(st_flag and kb == 0), stop=False, skip_group_check=True)
                    if g < 3:
                        e.matmul(ZZ[:, col:col + 1], lhsT=ones_b[0:1, :], rhs=pss[0:1, 2:3], start=False, stop=False, skip_group_check=True)
                        r = e.matmul(NQ[:, g * 4 + i:g * 4 + i + 1], lhsT=ones_b[0:1, :], rhs=pss[0:1, 2:3], start=True, stop=True, skip_group_check=True)
                    return r
                T.op("pe", fo, [pssb, vcbb, B_ones], [B_OP, B_ZZ, B_NQ])
                first[0] = False
                if g < 3:
                    T.op("dve", lambda e, g=g, i=i: e.scalar_tensor_tensor(out=OSF[:, i:i + 1], in0=VSf[g][0][:, i:i + 1], scalar=NQ[:, g * 4 + i:g * 4 + i + 1], in1=OSF[:, i:i + 1],
                                                                      op0=ALU.mult, op1=ALU.add), [B_NQ, VSf[g][1], B_OSF], [B_OSF])
        rq, rqb = RQ.get()
        T.op("act", lambda e: e.activation(out=rq[:, 0:16], in_=ZZ[:, 0:16], func=AF.Ln), [B_ZZ], [rqb])
        T.op("act", lambda e: e.activation(out=rq[:, 0:16], in_=rq[:, 0:16], func=AF.Exp, scale=-1.0), [rqb], [rqb])
        T.op("dve", lambda e: e.memset(yb[:].rearrange("p j t -> p (j t)"), 0.0), [], [ybb])
        nf, nfb = NF.get()
        T.op("dve", lambda e: e.tensor_tensor(out=nf[:, 0:NS], in0=OP[:, 0:NS], in1=OSF[:, 0:NS], op=ALU.add), [B_OP, B_OSF], [nfb])
        T.op("dve", lambda e: e.tensor_tensor(out=nf[:, 0:NS], in0=nf[:, 0:NS], in1=rq[:, 0:NS], op=ALU.mult), [nfb, rqb], [nfb])
        T.op("dve", lambda e: e.tensor_tensor(out=yb[:, 1, 0:NS], in0=nf[:, 0:NS], in1=gates[1][0][:, 0:NS], op=ALU.mult), [nfb, gates[1][1]], [ybb])
        T.op("dve", lambda e: e.tensor_tensor(out=nf[:, 8:8 + NS], in0=OP[:, 8:8 + NS], in1=rq[:, 8:8 + NS], op=ALU.mult), [B_OP, rqb], [nfb])
        T.op("dve", lambda e: e.tensor_tensor(out=yb[:, 2, 0:NS], in0=nf[:, 8:8 + NS], in1=gates[2][0][:, 0:NS], op=ALU.mult), [nfb, gates[2][1]], [ybb])
        for i in range(NS):
            T.dma(lambda e, i=i: e.dma_start(out=UE[:, i, 0:15], in_=d_sp[l, i, :, :].rearrange("r c -> c r"), allow_slow_non_contiguous=True), sl_kc, writes=[B_UE])
        T.op("dve", lambda e: e.tensor_copy(out=UE[:, :, 15], in_=ubuf[:, 16:16 + NS]), [B_u, B_UE], [B_UE])
        for i in range(NS):
            T.op("dve", lambda e, i=i: e.tensor_tensor(out=UP[:, i, :], in0=UE[:, i, :], in1=vcol(V_WM, 16), op=ALU.mult), [B_UE, B_vecs], [B_UP])
        T.op("dve", lambda e: e.reduce_sum(out=USUM[:], in_=UP[:], axis=AX.X), [B_UP], [B_US])
        dtb, dtbb = DTb
        T.op("dve", lambda e: e.memset(dtb[:, 0:SW], 0.0), [], [dtbb])
        T.op("dve", lambda e: e.tensor_tensor(out=dtb[:, 0:NS], in0=USUM[:], in1=ubuf[:, 16:16 + NS], op=ALU.subtract), [B_US, B_u], [dtbb])
        zp, zb = ZP.get()
        T.op("pe", lambda e, zp=zp: e.matmul(zp[:, 0:w], lhsT=pw[:], rhs=dtb[:, 0:w], start=True, stop=True), [B_pw, dtbb], [zb])
        T.op("dve", lambda e, zp=zp: e.scalar_tensor_tensor(out=yb[:, 0, 0:NS], in0=zp[:, 0:NS], scalar=vcol(vb + V_PS), in1=gates[0][0][:, 0:NS], op0=ALU.mult, op1=ALU.mult),
             [zb, B_vecs, gates[0][1]], [ybb])
        for i in range(NS):
            T.dma(lambda e, i=i: e.dma_start(out=o_sps[l, i, :, :].rearrange("r c -> c r"), in_=UE[:, i, 1:16], allow_slow_non_contiguous=True), sl_out, reads=[B_UE])
        T.op("pool", lambda e: e.memset(ubuf[:, 0:16], 0.0), [], [B_u])

    def phase2(l, tt):
        sample = (tt == NTT)
        w = SW if sample else TT
        st = 4 if sample else tt // 2
        off = 0 if sample else (tt % 2) * TT
        xoff = L if sample else tt * TT
        last = (l == depth - 1)
        T.dma(lambda e: e.dma_start(out=YT[:, :, 0:w], in_=yg_out[l][st].rearrange("(k p) t -> p k t", p=128)[:, :, off:off + w]), sl_y, reads=[B_ygo[l][st]], writes=[B_YT])
        T.dma(lambda e: e.dma_start(out=XR[:, :, 0:w], in_=xres_v[:, :, xoff:xoff + w]), sl_x, reads=[B_xres[tt]], writes=[B_XR])
        for c in range(4):
            zp, zb = ZP.get()

            def f(e, zp=zp, c=c):
                r = None
                for k in range(12):
                    r = e.matmul(zp[:, 0:w], lhsT=wo[:, k, c * 128:(c + 1) * 128], rhs=YT[:, k, 0:w], start=(k == 0), stop=(k == 11))
                return r
            T.op("pe", f, [B_wo, B_YT], [zb])
            T.op("dve", lambda e, zp=zp, c=c: e.tensor_tensor(out=XR[:, c, 0:w], in0=zp[:, 0:w], in1=XR[:, c, 0:w], op=ALU.add), [zb, B_XR], [B_XR])
        if not last:
            T.dma(lambda e: e.dma_start(out=xres_v[:, :, xoff:xoff + w], in_=XR[:, :, 0:w]), sl_x, reads=[B_XR], writes=[B_xres[tt]])
            T.op("act", lambda e: e.copy(out=XB[:, :, 0:w], in_=XR[:, :, 0:w]), [B_XR], [B_XB])
            T.dma(lambda e: e.dma_start(out=xg_in[l + 1][st].rearrange("(c p) t -> p c t", p=128)[:, :, off:off + w], in_=XB[:, :, 0:w]), sl_x,
                  reads=[B_XB], writes=[B_xgi[l + 1][st]])
            if sample or tt % 2 == 1:
                ag(xg_in[l + 1][st], xg_out[l + 1][st], B_xgi[l + 1][st], B_xgo[l + 1][st])
        else:
            nb = 1 if sample else 4
            for j in range(nb):
                cw = NS if sample else 128

                def f(e, j=j, cw=cw):
                    r = None
                    for c in range(4):
                        r = e.transpose(out=TP[0:cw, c * 128:(c + 1) * 128], in_=XR[:, c, j * 128:j * 128 + cw], identity=ident_f[:])
                    return r
                T.op("pe", f, [B_XR, B_idf], [B_TP])
                ct, cb = CST.get()
                T.op("act", lambda e, ct=ct, cw=cw: e.copy(out=ct[0:cw, :], in_=TP[0:cw, :]), [B_TP], [cb])
                if sample:
                    T.dma(lambda e, ct=ct: e.dma_start(out=o_ys[:, :], in_=ct[0:NS, :]), sl_out, reads=[cb])
                else:
                    T.dma(lambda e, ct=ct, j=j: e.dma_start(out=o_yp[tt * TT + j * 128: tt * TT + (j + 1) * 128, :], in_=ct[:, :]), sl_out, reads=[cb])

    load_w_out(0)
    for l in range(depth):
        load_w_in(l)
        mem_kv(l)
        order = [("1", 0), ("1", 1), ("1", 2), ("1", 3), ("2", 0), ("2", 1), ("1", 4), ("1", 5), ("2", 2), ("2", 3),
                 ("1", 6), ("1", 7), ("1", 8), ("2", 4), ("2", 5), ("2", 6), ("2", 7), ("2", 8)]
        for ph, tt in order:
            if ph == "1":
                phase1(l, tt)
            else:
                phase2(l, tt)
        if l + 1 < depth:
            load_w_out(l + 1)

    with nc.Block() as block:
        @block.tensor
        def _(h):
            T.replay("pe", h)

        @block.scalar
        def _(h):
            T.replay("act", h)

        @block.vector
        def _(h):
            T.replay("dve", h)

        @block.gpsimd
        def _(h):
            T.replay("pool", h)

        @block.sync
        def _(h):
            T.replay("sp", h, final=True)
    es.close()
    return nc


def _slot_cols(s):
    cols = []
    cols += list(range(s * 128, s * 128 + 128))
    cols += list(range(512 + s * 128, 512 + s * 128 + 128))
    c0 = 1024
    for g in range(3):
        for j in range(3):
            base = c0 + ((g * 3 + j) * 4 + s) * 128
            cols += list(range(base, base + 128))
    c1 = c0 + 9 * 512
    cols += list(range(c1 + s * 128, c1 + s * 128 + 128))
    c2 = c1 + 512
    cols += list(range(c2 + s * 128, c2 + s * 128 + 128))
    c3 = c2 + 512
    cols += list(range(c3 + s * 128, c3 + s * 128 + 128))
    return np.array(cols)


def _masks():
    kb = np.arange(128)[:, None]
    a = np.arange(128)[None, :]
    cur = (kb <= a).astype(np.float32)
    prev = (a <= kb).astype(np.float32)
    out = [np.tile(prev, (1, 4)), np.tile(cur, (1, 4))]
    a32 = np.arange(32)[None, :]
    for j in range(4):
        q = 32 * j + a32
        out.append(np.tile((q <= kb).astype(np.float32), (1, 16)))
        out.append(np.tile((kb <= q).astype(np.float32), (1, 16)))
    out.append(np.zeros((128, 512), np.float32))
    p0 = np.tile(prev, (1, 4))
    p0[:, 0:128] = 0.0
    out.append(p0)
    return np.concatenate(out, axis=1)


_NC_CACHE = {}


def kernel(x_prompt, x_sample, state_pool, cache_dil_w128, cache_dil_w512, cache_dil_w2048,
           cache_mem_kv, mem_prompt, norm_g, w_in, pool_w, pool_scale, dil_q_norm, dil_k_norm,
           mem_norm_g, w_mem_kv, mem_q_norm, mem_k_norm, w_out, _depth=DEPTH, _debug=False):
    f = lambda a: np.ascontiguousarray(np.asarray(a, dtype=np.float32))
    x_prompt, x_sample, state_pool = f(x_prompt), f(x_sample), f(state_pool)
    caches = [f(cache_dil_w128), f(cache_dil_w512), f(cache_dil_w2048)]
    cache_mem_kv, mem_prompt, norm_g, w_in, pool_w = f(cache_mem_kv), f(mem_prompt), f(norm_g), f(w_in), f(pool_w)
    pool_scale, dil_q_norm, dil_k_norm, mem_norm_g = f(pool_scale), f(dil_q_norm), f(dil_k_norm), f(mem_norm_g)
    w_mem_kv, mem_q_norm, mem_k_norm, w_out = f(w_mem_kv), f(mem_q_norm), f(mem_k_norm), f(w_out)
    depth = _depth
    if (depth, _debug) not in _NC_CACHE:
        _NC_CACHE[(depth, _debug)] = build(depth, _debug)
    nc = _NC_CACHE[(depth, _debug)]
    ident = np.eye(128, dtype=np.float32)
    masks = _masks()
    in_maps = []
    for c in range(8):
        b, s = c // 4, c % 4
        w = 2 ** (s + 1)
        vecs = np.zeros((128, NV), np.float32)
        for l in range(DEPTH):
            vb = l * VL
            vecs[:, vb + V_NG: vb + V_NG + 16] = norm_g[l].reshape(16, 128).T
            vecs[:, vb + V_MG: vb + V_MG + 16] = mem_norm_g[l].reshape(16, 128).T
            vecs[:, vb + V_PS] = pool_scale[l, s * 128:(s + 1) * 128]
            for g in range(3):
                vecs[:, vb + V_QN + g] = dil_q_norm[l, g]
                vecs[:, vb + V_KN + g] = dil_k_norm[l, g]
            vecs[:, vb + V_MQ] = mem_q_norm[l]
        vecs[:, V_SEL + s] = 1.0
        wm = np.zeros(16, np.float32)
        wm[16 - w:] = 1.0 / w
        vecs[:, V_WM:V_WM + 16] = wm[None, :]
        vecs[:, V_CI:V_CI + 16] = (1.0 / np.minimum(w, np.arange(16) + 1.0))[None, :]
        vecs[:, V_IW] = 1.0 / w
        rows = np.concatenate([np.arange(br * 512 + r * 128, br * 512 + r * 128 + 128) for r in range(4) for br in range(3)])
        m = {
            "xin": np.ascontiguousarray(x_prompt[b][:, s * 512:(s + 1) * 512]),
            "xs": np.ascontiguousarray(x_sample[4 * b:4 * b + 4, 0, s * 512:(s + 1) * 512]),
            "vecs": vecs,
            "mkn": np.ascontiguousarray(np.broadcast_to(mem_k_norm.reshape(1, DEPTH * 128), (128, DEPTH * 128))),
            "ident": ident,
            "masks": masks,
            "win": np.ascontiguousarray(w_in[:, :, _slot_cols(s)]),
            "wout": np.ascontiguousarray(w_out[:, rows][:, :, s * 512:(s + 1) * 512]),
            "wm": np.ascontiguousarray(np.concatenate([w_mem_kv[:, :, s * 128:(s + 1) * 128], w_mem_kv[:, :, 512 + s * 128:512 + (s + 1) * 128]], axis=2)),
            "pw": np.ascontiguousarray(pool_w[:, s]),
            "mem": np.ascontiguousarray(mem_prompt[b]),
            "cm": np.ascontiguousarray(cache_mem_kv[:, 4 * b:4 * b + 4, :, :, s, :]),
            "spool": np.ascontiguousarray(state_pool[:, 4 * b:4 * b + 4, :, s * 128:(s + 1) * 128]),
        }
        for g in range(3):
            m[f"ck{g}"] = np.ascontiguousarray(caches[g][:, 4 * b:4 * b + 4, :, :, s, :])
        in_maps.append(m)
    res = run_bass_kernel_spmd(nc, in_maps, core_ids=list(range(8)))
    R = res.results
    y_prompt = np.zeros((2, L, D), np.float32)
    y_sample = np.zeros((8, 1, D), np.float32)
    sp_p = np.zeros((DEPTH, 2, 15, 512), np.float32)
    cp = [np.zeros((DEPTH, 2, W, 2, 4, 128), np.float32) for W in WINS]
    cmp_ = np.zeros((DEPTH, 2, 256, 2, 4, 128), np.float32)
    sp_s = np.zeros((DEPTH, 8, 15, 512), np.float32)
    cs = [np.zeros((DEPTH, 8, W, 2, 4, 128), np.float32) for W in WINS]
    for c in range(8):
        b, s = c // 4, c % 4
        r = R[c]
        y_prompt[b][:, s * 512:(s + 1) * 512] = r["o_yp"]
        y_sample[4 * b:4 * b + 4, 0, s * 512:(s + 1) * 512] = r["o_ys"]
        sp_p[:, b, :, s * 128:(s + 1) * 128] = r["o_spp"]
        cmp_[:, b, :, :, s, :] = r["o_cmp"]
        sp_s[:, 4 * b:4 * b + 4, :, s * 128:(s + 1) * 128] = r["o_sps"]
        for g in range(3):
            cp[g][:, b, :, :, s, :] = r[f"o_cp{g}"]
            cs[g][:, 4 * b:4 * b + 4, :, :, s, :] = r[f"o_cs{g}"]
    return (y_prompt, y_sample, sp_p, cp[0], cp[1], cp[2], cmp_, sp_s, cs[0], cs[1], cs[2])
```

```python
import numpy as np
import ml_dtypes
from contextlib import ExitStack
import concourse.bass as bass
import concourse.mybir as mybir
from concourse.bass_utils import run_bass_kernel_spmd

F32, BF16 = mybir.dt.float32, mybir.dt.bfloat16
AF = mybir.ActivationFunctionType
ALU = mybir.AluOpType
AX = mybir.AxisListType

DEPTH = 4
D = 2048
L = 4096
TT = 512
NTT = 8
NCH = 16
SLOTC = 1792
NS = 4
SW = 16
WINS = (128, 512, 2048)
DILS = (1, 4, 16)
EPS = 1e-6
SCALE = 128 ** -0.5
GROUPS = [[0, 1, 2, 3], [4, 5, 6, 7]]
V_NG, V_MG, V_PS, V_QN, V_KN, V_MQ, VL = 0, 16, 32, 33, 36, 39, 40
V_SEL = DEPTH * VL
V_WM = V_SEL + 4
V_CI = V_WM + 16
V_IW = V_CI + 16
NV = V_IW + 1


class Buf:
    __slots__ = ("w", "r")

    def __init__(self):
        self.w = {}
        self.r = {}


class Eng:
    def __init__(self, name, sem):
        self.name, self.sem, self.cnt, self.waited, self.plan = name, sem, 0, {}, []


class Slot:
    def __init__(self, sems, ring):
        self.sems, self.ring = sems, ring
        self.cnts = [0] * len(sems)
        self.i = 0


class Tracker:
    def __init__(self, nc, es):
        self.nc, self.es = nc, es
        self.engs = {n: Eng(n, es.enter_context(nc.semaphore("e_" + n))) for n in ("pe", "act", "dve", "pool", "sp")}
        self.slots = []

    def slot(self, name, ring=4):
        n = max(ring, 1)
        sl = Slot([self.es.enter_context(self.nc.semaphore(f"d_{name}{k}")) for k in range(n)], ring)
        self.slots.append(sl)
        return sl

    def _issue(self, e, slot, inc):
        q = slot.i % len(slot.sems)
        slot.i += 1
        sem = slot.sems[q]
        if slot.ring and slot.cnts[q] and e.waited.get(sem, 0) < slot.cnts[q]:
            e.plan.append(("w", sem, slot.cnts[q]))
            e.waited[sem] = slot.cnts[q]
        slot.cnts[q] += inc
        return sem, slot.cnts[q]

    def _waits(self, e, reads, writes):
        deps = []
        for b in reads:
            deps.extend(b.w.items())
        for b in writes:
            deps.extend(b.w.items())
            deps.extend(b.r.items())
        for sem, val in deps:
            if e.name == "pe" and sem is e.sem:
                continue
            if e.waited.get(sem, 0) < val:
                e.plan.append(("w", sem, val))
                e.waited[sem] = val

    def _mark(self, dep, reads, writes):
        for b in reads:
            if b.r.get(dep[0], 0) < dep[1]:
                b.r[dep[0]] = dep[1]
        for b in writes:
            if b.w.get(dep[0], 0) < dep[1]:
                b.w[dep[0]] = dep[1]

    def op(self, en, fn, reads=(), writes=()):
        e = self.engs[en]
        self._waits(e, reads, writes)
        e.cnt += 1
        e.plan.append(("o", fn, e.sem, 1))
        self._mark((e.sem, e.cnt), reads, writes)

    def dma(self, fn, slot, reads=(), writes=(), n=1, en="sp"):
        e = self.engs[en]
        self._waits(e, reads, writes)
        sem, val = self._issue(e, slot, 16)
        e.plan.append(("o", fn, sem, 16))
        self._mark((sem, val), reads, writes)

    def cc(self, fn, slot, reads=(), writes=()):
        e = self.engs["pool"]
        self._waits(e, reads, writes)
        sem, val = self._issue(e, slot, 1)
        e.plan.append(("o", fn, sem, 1))
        self._mark((sem, val), reads, writes)

    def replay(self, en, h, final=False):
        e = self.engs[en]
        for it in e.plan:
            if it[0] == "w":
                h.wait_ge(it[1], it[2])
            else:
                r = it[1](h)
                for ins in (r if isinstance(r, (list, tuple)) else [r]):
                    ins.then_inc(it[2], it[3])
        if final:
            for sl in self.slots:
                for sem, cnt in zip(sl.sems, sl.cnts):
                    if cnt:
                        h.wait_ge(sem, cnt)
            for o in self.engs.values():
                if o.cnt and o is not e:
                    h.wait_ge(o.sem, o.cnt)


class Rot:
    def __init__(self, tiles):
        self.tiles, self.i = tiles, 0

    def get(self):
        t = self.tiles[self.i % len(self.tiles)]
        self.i += 1
        return t


def build(depth=DEPTH, debug=False):
    nc = bass.Bass("TRN2", target_bir_lowering=False)
    es = ExitStack()
    T = Tracker(nc, es)

    def dram(name, shape, dtype=F32, kind="ExternalInput"):
        return nc.dram_tensor(name, list(shape), dtype, kind=kind).ap()

    d_xin = dram("xin", [L, 512])
    d_xs = dram("xs", [NS, 512])
    d_vecs = dram("vecs", [128, NV])
    d_mkn = dram("mkn", [128, DEPTH * 128])
    d_ident = dram("ident", [128, 128])
    d_masks = dram("masks", [128, 12 * 512])
    d_win = dram("win", [DEPTH, D, SLOTC])
    d_wout = dram("wout", [DEPTH, 1536, 512])
    d_wm = dram("wm", [DEPTH, D, 256])
    d_pw = dram("pw", [DEPTH, 128, 128])
    d_mem = dram("mem", [256, D])
    d_ck = [dram(f"ck{g}", [DEPTH, NS, WINS[g], 2, 128]) for g in range(3)]
    d_cm = dram("cm", [DEPTH, NS, 256, 2, 128])
    d_sp = dram("spool", [DEPTH, NS, 15, 128])
    EO = "ExternalOutput"
    o_yp = dram("o_yp", [L, 512], kind=EO)
    o_ys = dram("o_ys", [NS, 512], kind=EO)
    o_spp = dram("o_spp", [DEPTH, 15, 128], kind=EO)
    o_cp = [dram(f"o_cp{g}", [DEPTH, WINS[g], 2, 128], kind=EO) for g in range(3)]
    o_cmp = dram("o_cmp", [DEPTH, 256, 2, 128], kind=EO)
    o_sps = dram("o_sps", [DEPTH, NS, 15, 128], kind=EO)
    o_cs = [dram(f"o_cs{g}", [DEPTH, NS, WINS[g], 2, 128], kind=EO) for g in range(3)]
    if debug:
        dbg_od = dram("dbg_od", [128, 512], kind=EO)
        dbg_rz = dram("dbg_rz", [128, 512], kind=EO)
        dbg_y = dram("dbg_y", [128, 3, 512], BF16, kind=EO)
        dbg_q = dram("dbg_q", [128, 3, 512], BF16, kind=EO)
    IN = "Internal"
    xres_d = dram("xres_d", [512, L + SW], kind=IN)
    xg_in = [[dram(f"xgi{l}_{st}", [512, 1024], BF16, IN) for st in range(4)] + [dram(f"xgi{l}_s", [512, SW], BF16, IN)] for l in range(depth)]
    xg_out = [[dram(f"xgo{l}_{st}", [2048, 1024], BF16, IN) for st in range(4)] + [dram(f"xgo{l}_s", [2048, SW], BF16, IN)] for l in range(depth)]
    yg_in = [[dram(f"ygi{l}_{st}", [384, 1024], BF16, IN) for st in range(4)] + [dram(f"ygi{l}_s", [384, SW], BF16, IN)] for l in range(depth)]
    yg_out = [[dram(f"ygo{l}_{st}", [1536, 1024], BF16, IN) for st in range(4)] + [dram(f"ygo{l}_s", [1536, SW], BF16, IN)] for l in range(depth)]
    B_xgi = [[Buf() for _ in range(5)] for _ in range(depth)]
    B_xgo = [[Buf() for _ in range(5)] for _ in range(depth)]
    B_ygi = [[Buf() for _ in range(5)] for _ in range(depth)]
    B_ygo = [[Buf() for _ in range(5)] for _ in range(depth)]
    B_xres = [Buf() for _ in range(NTT + 1)]

    def sb(name, shape, dtype=F32):
        return es.enter_context(nc.sbuf_tensor(name, list(shape), dtype)), Buf()

    def ps(name):
        return es.enter_context(nc.psum_tensor(name, [128, 512], F32)), Buf()

    ident_f, B_idf = sb("ident_f", [128, 128])
    ident_b, B_idb = sb("ident_b", [128, 128], BF16)
    ones_b, B_ones = sb("ones_b", [128, 128], BF16)
    vecs, B_vecs = sb("vecs_t", [128, NV])
    mkn, B_mkn = sb("mkn_t", [128, 128])
    masks, B_masks = sb("masks_t", [128, 12 * 512], BF16)
    wi, B_wi = sb("wi", [128, NCH, SLOTC], BF16)
    wo, B_wo = sb("wo", [128, 12, 512], BF16)
    pw, B_pw = sb("pw_t", [128, 128], BF16)
    KmT, B_KmT = sb("KmT", [128, 256], BF16)
    Vm, B_Vm = sb("Vm", [128, 2, 128], BF16)
    KSL = (8, 8, 32)
    kT = [sb(f"kT{g}", [128, KSL[g] * 128], BF16)[0] for g in range(3)]
    B_k = [[Buf() for _ in range(KSL[g])] for g in range(3)]
    vT2, B_vT2 = sb("vT2", [128, L], BF16)
    Vb = [sb(f"Vb{g}", [128, KSL[g], 128], BF16)[0] for g in range(3)]
    B_v = [[Buf() for _ in range(KSL[g])] for g in range(3)]
    HT0 = sb("hT0", [128, NCH, TT], BF16)
    HT = [HT0, HT0]
    stage = sb("stage", [128, SLOTC // 2])
    ubuf, B_u = sb("ubuf", [128, 16 + TT])
    rstdx, B_rx = sb("rstdx", [128, TT])
    gates = [sb(f"gate{i}", [128, TT], BF16) for i in range(3)]
    QD = [sb(f"qd{g}", [128, TT], BF16) for g in range(3)]
    QM = sb("qm", [128, TT], BF16)
    VD = Rot([sb(f"vd{i}", [128, TT], BF16) for i in range(1)])
    SQ = Rot([sb(f"sq{i}", [128, TT], BF16) for i in range(2)])
    ZT = Rot([sb(f"zt{i}", [128, TT]) for i in range(2)])
    NF = Rot([sb(f"nf{i}", [128, TT]) for i in range(2)])
    RQ = Rot([sb(f"rq{i}", [128, TT]) for i in range(1)])
    PT = Rot([sb(f"pt{i}", [128, TT], BF16) for i in range(2)])
    YB = Rot([sb(f"yb{i}", [128, 3, TT], BF16) for i in range(1)])
    PA = sb("pa", [128, 16 + TT])
    PB = sb("pb", [128, 16 + TT])
    PACC = sb("pacc", [128, TT])
    DTb = sb("dtb", [128, TT], BF16)
    CST = Rot([sb(f"cst{i}", [128, 512]) for i in range(2)])
    YT, B_YT = sb("yT", [128, 12, TT], BF16)
    XR, B_XR = sb("xr", [128, 4, TT])
    XB, B_XB = sb("xb", [128, 4, TT], BF16)
    wmk, B_wmk = YT[:].rearrange("p k t -> p (k t)")[:, 0:NCH * 256].rearrange("p (c n) -> p c n", c=NCH), B_YT
    memT, B_memT = HT0[0][:].rearrange("p c t -> p (c t)")[:, 0:NCH * 256].rearrange("p (c n) -> p c n", c=NCH), HT0[1]
    memt, B_memt = XR[:].rearrange("p c t -> p (c t)"), B_XR
    UE, B_UE = sb("ue", [128, NS, 16])
    UP, B_UP = sb("up", [128, NS, 16])
    USUM, B_US = sb("usum", [128, NS])
    KSf = [sb(f"ksf{g}", [128, SW]) for g in range(3)]
    KSb = [sb(f"ksb{g}", [128, SW], BF16) for g in range(3)]
    VSf = [sb(f"vsf{g}", [128, SW]) for g in range(3)]
    KC = Rot([sb(f"kc{i}", [128, 2, 2, 128]) for i in range(1)])
    KCT = Rot([sb(f"kct{i}", [128, 256], BF16) for i in range(2)])
    VCb = Rot([sb(f"vcb{i}", [128, 2, 128], BF16) for i in range(2)])
    PSs = Rot([sb(f"pss{i}", [128, 4], BF16) for i in range(4)])
    PSB, B_PSB = sb("psb", [128, 16])
    OSF, B_OSF = sb("osf", [128, SW])
    TMP4, B_TMP4 = sb("tmp4", [128, SW])
    small = Rot([sb(f"sm{i}", [128, 8]) for i in range(4)])
    ZP = Rot([ps("zp0"), ps("zp1")])
    SP = Rot([ps("sp0"), ps("sp1")])
    OP, B_OP = ps("op")
    ZZ, B_ZZ = ps("zz")
    TP, B_TP = ps("tp")
    NQ, B_NQ = ps("nq")
    TPb = TP[:].bitcast(BF16)

    sl_misc = T.slot("misc")
    sl_h = [T.slot("h0", ring=1), T.slot("h1", ring=1)]
    sl_w = T.slot("w")
    sl_out = T.slot("out", ring=8)
    sl_x = T.slot("x")
    sl_y = T.slot("y")
    sl_cc = T.slot("ccx")
    sl_ccy = T.slot("ccy")
    sl_bulk = T.slot("bulk", ring=0)
    sl_kc = T.slot("kc")

    def vcol(c, n=1):
        return vecs[:, c:c + n]

    def ld(dst, src, bw, slot=sl_misc, reads=(), **kw):
        T.dma(lambda e: e.dma_start(out=dst, in_=src, **kw), slot, reads=reads, writes=bw)

    ld(vecs[:], d_vecs[:, :], [B_vecs])
    ld(ident_f[:], d_ident[:, :], [B_idf])
    T.op("dve", lambda e: e.tensor_copy(out=ident_b[:], in_=ident_f[:]), [B_idf], [B_idb])
    T.op("dve", lambda e: e.memset(ones_b[:], 1.0), [], [B_ones])
    for i in range(12):
        st_t, st_b = stage
        ld(st_t[:, 0:512], d_masks[:, i * 512:(i + 1) * 512], [st_b])
        T.op("dve", lambda e, i=i: e.tensor_copy(out=masks[:, i * 512:(i + 1) * 512], in_=stage[0][:, 0:512]), [st_b], [B_masks])
    allk = [b for g in range(3) for b in B_k[g]]
    allv = [b for g in range(3) for b in B_v[g]]
    for g in range(3):
        T.op("dve", lambda e, g=g: e.memset(kT[g][:], 0.0), [], B_k[g])
        T.op("dve", lambda e, g=g: e.memset(Vb[g][:].rearrange("p a b -> p (a b)"), 0.0), [], B_v[g])
    T.op("dve", lambda e: e.memset(vT2[:], 0.0), [], [B_vT2])
    T.op("dve", lambda e: e.memset(ubuf[:], 0.0), [], [B_u])

    def transpose_f32(src_ap, cols, dst_col):
        return lambda e: e.transpose(out=TP[0:cols, dst_col:dst_col + 128], in_=src_ap, identity=ident_f[:])

    xres_v = xres_d.rearrange("(c p) t -> p c t", p=128)

    def x_init_tile(tt, w, src_rows):
        nb = 4 if w == TT else 1
        for j in range(nb):
            ct, cb = CST.get()
            if w == TT:
                ld(ct[:], d_xin[tt * TT + j * 128: tt * TT + (j + 1) * 128, :], [cb], slot=sl_x)
            else:
                T.op("dve", lambda e, ct=ct: e.memset(ct[:], 0.0), [], [cb])
                ld(ct[0:NS, :], d_xs[:, :], [cb], slot=sl_x)

            def f(e, ct=ct):
                r = None
                for c in range(4):
                    r = e.transpose(out=TP[:, c * 128:(c + 1) * 128], in_=ct[:, c * 128:(c + 1) * 128], identity=ident_f[:])
                return r
            T.op("pe", f, [cb, B_idf], [B_TP])
            cw = 128 if w == TT else SW
            T.op("dve", lambda e, j=j, cw=cw: e.tensor_copy(out=XR[:, :, j * 128:j * 128 + cw], in_=TP[:].rearrange("p (c t) -> p c t", c=4)[:, :, 0:cw]), [B_TP], [B_XR])
        T.op("act", lambda e: e.copy(out=XB[:, :, 0:w], in_=XR[:, :, 0:w]), [B_XR], [B_XB])
        off = tt * TT if w == TT else L
        T.dma(lambda e: e.dma_start(out=xres_v[:, :, off:off + w], in_=XR[:, :, 0:w]), sl_x, reads=[B_XR], writes=[B_xres[tt]])
        st = tt // 2 if w == TT else 4
        o2 = (tt % 2) * TT if w == TT else 0
        T.dma(lambda e: e.dma_start(out=xg_in[0][st].rearrange("(c p) t -> p c t", p=128)[:, :, o2:o2 + w], in_=XB[:, :, 0:w]), sl_x,
              reads=[B_XB], writes=[B_xgi[0][st]])

    def ag(kind_in, kind_out, Bi, Bo, slot=None):
        T.cc(lambda e: e.collective_compute("AllGather", ALU.bypass, replica_groups=GROUPS, ins=[kind_in], outs=[kind_out]),
             slot or sl_cc, reads=[Bi], writes=[Bo])

    x_init_tile(0, TT, None)
    for tt in range(NTT):
        x_init_tile(tt, TT, None)
        if tt % 2 == 1:
            ag(xg_in[0][tt // 2], xg_out[0][tt // 2], B_xgi[0][tt // 2], B_xgo[0][tt // 2])
    x_init_tile(NTT, SW, None)
    ag(xg_in[0][4], xg_out[0][4], B_xgi[0][4], B_xgo[0][4])

    for g in range(3):
        W = WINS[g]
        for l in range(depth):
            for i in range(NS):
                T.dma(lambda e, g=g, l=l, i=i, W=W: e.dma_start(
                    out=o_cs[g][l, i, 0:W - 1, :, :].rearrange("w j d -> (w j d)").rearrange("(a b) -> a b", b=256),
                    in_=d_ck[g][l, i, 1:W, :, :].rearrange("w j d -> (w j d)").rearrange("(a b) -> a b", b=256)), sl_bulk)

    def load_w_in(l):
        vb = l * VL
        st_t, st_b = stage
        HW = SLOTC // 2
        for c in range(NCH):
            for hf in range(2):
                ld(st_t[:], d_win[l, c * 128:(c + 1) * 128, hf * HW:(hf + 1) * HW], [st_b], slot=sl_w)
                T.op("dve", lambda e, c=c, hf=hf: e.tensor_scalar(out=wi[:, c, hf * HW:(hf + 1) * HW], in0=stage[0][:], scalar1=vcol(vb + V_NG + c), scalar2=None, op0=ALU.mult),
                     [st_b, B_vecs], [B_wi])
        for c in range(NCH):
            ld(st_t[:, 0:256], d_wm[l, c * 128:(c + 1) * 128, :], [st_b], slot=sl_w)
            T.op("dve", lambda e, c=c: e.tensor_scalar(out=wmk[:, c, :], in0=stage[0][:, 0:256], scalar1=vcol(vb + V_MG + c), scalar2=None, op0=ALU.mult),
                 [st_b, B_vecs], [B_wmk])
        ld(st_t[:, 0:128], d_pw[l, :, :], [st_b], slot=sl_w)
        T.op("dve", lambda e: e.tensor_copy(out=pw[:], in_=stage[0][:, 0:128]), [st_b], [B_pw])

    def load_w_out(l):
        st_t, st_b = stage
        for k in range(12):
            ld(st_t[:, 0:512], d_wout[l, k * 128:(k + 1) * 128, :], [st_b], slot=sl_w)
            T.op("dve", lambda e, k=k: e.tensor_copy(out=wo[:, k, :], in_=stage[0][:, 0:512]), [st_b], [B_wo])

    def mem_kv(l):
        ld(mkn[:], d_mkn[:, l * 128:(l + 1) * 128], [B_mkn])
        for mb in range(2):
            ld(memt[:], d_mem[mb * 128:(mb + 1) * 128, :], [B_memt])
            sm, smb = small.get()
            for q in range(4):
                sq, sqb = SQ.get()
                T.op("act", lambda e, sm=sm, sq=sq, q=q: e.activation(out=sq[:, 0:512], in_=memt[:, q * 512:(q + 1) * 512], func=AF.Square, accum_out=sm[:, q:q + 1]), [B_memt], [sqb, smb])
            T.op("dve", lambda e, sm=sm: e.reduce_sum(out=sm[:, 4:5], in_=sm[:, 0:4], axis=AX.X), [smb], [smb])
            T.op("act", lambda e, sm=sm: e.activation(out=sm[:, 5:6], in_=sm[:, 4:5], func=AF.Ln, scale=1.0 / D, bias=EPS), [smb], [smb])
            T.op("act", lambda e, sm=sm: e.activation(out=sm[:, 6:7], in_=sm[:, 5:6], func=AF.Exp, scale=-0.5), [smb], [smb])
            T.op("dve", lambda e, sm=sm: e.tensor_scalar(out=memt[:], in0=memt[:], scalar1=sm[:, 6:7], scalar2=None, op0=ALU.mult), [smb, B_memt], [B_memt])
            for q in range(4):
                def f(e, q=q):
                    r = None
                    for j in range(4):
                        c = q * 4 + j
                        r = e.transpose(out=TP[:, j * 128:(j + 1) * 128], in_=memt[:, c * 128:(c + 1) * 128], identity=ident_f[:])
                    return r
                T.op("pe", f, [B_memt, B_idf], [B_TP])
                T.op("dve", lambda e, q=q, mb=mb: e.tensor_copy(out=memT[:, q * 4:(q + 1) * 4, mb * 128:(mb + 1) * 128],
                                                             in_=TP[:].rearrange("p (j t) -> p j t", j=4)), [B_TP], [B_memT])
        for mb in range(2):
            zp, zb = ZP.get()

            def f(e, mb=mb, zp=zp):
                r = None
                for c in range(NCH):
                    r = e.matmul(zp[:, 0:256], lhsT=memT[:, c, mb * 128:(mb + 1) * 128], rhs=wmk[:, c, :], start=(c == 0), stop=(c == NCH - 1))
                return r
            T.op("pe", f, [B_memT, B_wmk], [zb])
            ct, cb = CST.get()
            sm, smb = small.get()
            nf, nfb = NF.get()
            T.op("act", lambda e, zp=zp, nf=nf, sm=sm: e.activation(out=nf[:, 0:128], in_=zp[:, 0:128], func=AF.Square, accum_out=sm[:, 0:1]), [zb], [nfb, smb])
            T.op("act", lambda e, sm=sm: e.activation(out=sm[:, 1:2], in_=sm[:, 0:1], func=AF.Ln, scale=1.0 / 128, bias=EPS), [smb], [smb])
            T.op("act", lambda e, sm=sm: e.activation(out=sm[:, 2:3], in_=sm[:, 1:2], func=AF.Exp, scale=-0.5), [smb], [smb])
            T.op("dve", lambda e, zp=zp, ct=ct, sm=sm: e.scalar_tensor_tensor(out=ct[:, 0:128], in0=zp[:, 0:128], scalar=sm[:, 2:3], in1=mkn[:, :],
                                                                           op0=ALU.mult, op1=ALU.mult), [zb, smb, B_mkn], [cb])
            T.op("act", lambda e, zp=zp, ct=ct: e.copy(out=ct[:, 128:256], in_=zp[:, 128:256]), [zb], [cb])
            T.dma(lambda e, ct=ct, mb=mb: e.dma_start(out=o_cmp[l, mb * 128:(mb + 1) * 128, :, :], in_=ct[:, 0:256].rearrange("p (j d) -> p j d", j=2)), sl_out, reads=[cb])
            T.op("dve", lambda e, ct=ct, mb=mb: e.tensor_copy(out=Vm[:, mb, :], in_=ct[:, 128:256]), [cb], [B_Vm])
            T.op("pe", lambda e, ct=ct: e.transpose(out=TP[:, 0:128], in_=ct[:, 0:128], identity=ident_f[:]), [cb, B_idf], [B_TP])
            T.op("dve", lambda e, mb=mb: e.tensor_copy(out=KmT[:, mb * 128:(mb + 1) * 128], in_=TP[:, 0:128]), [B_TP], [B_KmT])

    def norm_rstd(src, srcb, w, n, from_psum_ssq=None):
        sq, sqb = SQ.get()
        T.op("pool", lambda e: e.tensor_tensor(out=sq[:, 0:w], in0=src, in1=src, op=ALU.mult), [srcb], [sqb])
        T.op("pe", lambda e: e.matmul(NQ[:, 0:w], lhsT=ones_b[:], rhs=sq[:, 0:w], start=True, stop=True), [sqb, B_ones], [B_NQ])
        rq, rqb = RQ.get()
        T.op("act", lambda e: e.activation(out=rq[:, 0:w], in_=NQ[:, 0:w], func=AF.Ln, scale=1.0 / n, bias=EPS), [B_NQ], [rqb])
        T.op("act", lambda e: e.activation(out=rq[:, 0:w], in_=rq[:, 0:w], func=AF.Exp, scale=-0.5), [rqb], [rqb])
        return rq, rqb

    def out_tokmajor(src, srcb, w, dst_fn):
        nb = (w + 127) // 128
        for j in range(nb):
            cw = min(128, w - j * 128)
            T.op("pe", lambda e, j=j, cw=cw: e.transpose(out=TP[0:cw, 0:128], in_=src[:, j * 128:j * 128 + cw], identity=ident_f[:]), [srcb, B_idf], [B_TP])
            ct, cb = CST.get()
            T.op("act", lambda e, ct=ct, cw=cw: e.copy(out=ct[0:cw, 0:128], in_=TP[0:cw, 0:128]), [B_TP], [cb])
            dst = dst_fn(j, cw)
            T.dma(lambda e, ct=ct, cw=cw, dst=dst: e.dma_start(out=dst, in_=ct[0:cw, 0:128]), sl_out, reads=[cb])

    def deint(ap, dil):
        return ap.rearrange("p (r a) -> p a r", r=dil)

    def nat(ap, dil):
        return ap.rearrange("p (a r) -> p a r", r=dil)

    def phase1(l, tt):
        sample = (tt == NTT)
        w = SW if sample else TT
        st = 4 if sample else tt // 2
        off = 0 if sample else (tt % 2) * TT
        vb = l * VL
        hbuf, hb = HT[tt % 2]
        T.dma(lambda e: e.dma_start(out=hbuf[:, :, 0:w], in_=xg_out[l][st].rearrange("(c p) t -> p c t", p=128)[:, :, off:off + w]),
              sl_h[tt % 2], reads=[B_xgo[l][st]], writes=[hb])
        for c in range(NCH):
            sq, sqb = SQ.get()
            T.op("pool", lambda e, c=c, sq=sq: e.tensor_tensor(out=sq[:, 0:w], in0=hbuf[:, c, 0:w], in1=hbuf[:, c, 0:w], op=ALU.mult), [hb], [sqb])
            T.op("pe", lambda e, c=c, sq=sq: e.matmul(NQ[:, 0:w], lhsT=ones_b[:], rhs=sq[:, 0:w], start=(c == 0), stop=(c == NCH - 1), skip_group_check=True),
                 [sqb, B_ones], [B_NQ])
        T.op("act", lambda e: e.activation(out=rstdx[:, 0:w], in_=NQ[:, 0:w], func=AF.Ln, scale=1.0 / D, bias=EPS), [B_NQ], [B_rx])
        T.op("act", lambda e: e.activation(out=rstdx[:, 0:w], in_=rstdx[:, 0:w], func=AF.Exp, scale=-0.5), [B_rx], [B_rx])

        def zchunk(j):
            zp, zb = ZP.get()

            def f(e):
                r = None
                for c in range(NCH):
                    r = e.matmul(zp[:, 0:w], lhsT=wi[:, c, j * 128:(j + 1) * 128], rhs=hbuf[:, c, 0:w], start=(c == 0), stop=(c == NCH - 1))
                return r
            T.op("pe", f, [B_wi, hb], [zb])
            return zp, zb

        def evac(zp, zb, dst, dstb):
            T.op("dve", lambda e: e.tensor_tensor(out=dst, in0=zp[:, 0:w], in1=rstdx[:, 0:w], op=ALU.mult), [zb, B_rx], [dstb])

        zp, zb = zchunk(0)
        evac(zp, zb, ubuf[:, 16:16 + w], B_u)
        for gi, j in enumerate((1, 11, 13)):
            zp, zb = zchunk(j)
            zt, ztb = ZT.get()
            evac(zp, zb, zt[:, 0:w], ztb)
            gt, gb = gates[gi]
            T.op("act", lambda e, zt=zt, gt=gt: e.activation(out=gt[:, 0:w], in_=zt[:, 0:w], func=AF.Silu), [ztb], [gb])
        for g in range(3):
            dil, W = DILS[g], WINS[g]
            zp, zb = zchunk(2 + 3 * g)
            zt, ztb = ZT.get()
            evac(zp, zb, zt[:, 0:w], ztb)
            rq, rqb = norm_rstd(zt[:, 0:w], ztb, w, 128)
            qd, qdb = QD[g]
            if sample:
                T.op("dve", lambda e, zt=zt, rq=rq, qd=qd, g=g: e.scalar_tensor_tensor(out=qd[:, 0:w], in0=zt[:, 0:w], scalar=vcol(vb + V_QN + g), in1=rq[:, 0:w],
                                                                                 op0=ALU.mult, op1=ALU.mult), [ztb, rqb, B_vecs], [qdb])
            else:
                T.op("dve", lambda e, zt=zt, rq=rq, qd=qd, g=g, dil=dil: e.scalar_tensor_tensor(out=deint(qd[:, 0:w], dil), in0=nat(zt[:, 0:w], dil), scalar=vcol(vb + V_QN + g),
                                                                                          in1=nat(rq[:, 0:w], dil), op0=ALU.mult, op1=ALU.mult), [ztb, rqb, B_vecs], [qdb])
            zp, zb = zchunk(3 + 3 * g)
            zt, ztb = ZT.get()
            evac(zp, zb, zt[:, 0:w], ztb)
            rq, rqb = norm_rstd(zt[:, 0:w], ztb, w, 128)
            if sample:
                kf, kfb = KSf[g]
                T.op("dve", lambda e, zt=zt, rq=rq, kf=kf, g=g: e.scalar_tensor_tensor(out=kf[:, 0:w], in0=zt[:, 0:w], scalar=vcol(vb + V_KN + g), in1=rq[:, 0:w],
                                                                                 op0=ALU.mult, op1=ALU.mult), [ztb, rqb, B_vecs], [kfb])
                T.op("act", lambda e, kf=kf, g=g: e.copy(out=KSb[g][0][:, 0:w], in_=kf[:, 0:w]), [kfb], [KSb[g][1]])
                T.dma(lambda e, kf=kf, g=g, W=W: e.dma_start(out=o_cs[g][l, :, W - 1, 0, :].rearrange("i d -> d i"), in_=kf[:, 0:NS], allow_slow_non_contiguous=True),
                      sl_out, reads=[kfb])
            else:
                nf, nfb = NF.get()
                T.op("dve", lambda e, zt=zt, rq=rq, nf=nf, g=g: e.scalar_tensor_tensor(out=nf[:, 0:w], in0=zt[:, 0:w], scalar=vcol(vb + V_KN + g), in1=rq[:, 0:w],
                                                                                 op0=ALU.mult, op1=ALU.mult), [ztb, rqb, B_vecs], [nfb])
                na = TT // dil
                i0 = (tt % 2) * na if g < 2 else tt * na
                kview = kT[g][:].rearrange("p (r i) -> p i r", r=dil)[:, i0:i0 + na, :]
                kblocks = blocks_of(g, tt)
                T.op("act", lambda e, nf=nf, kview=kview, dil=dil: e.copy(out=kview, in_=nat(nf[:, 0:w], dil)), [nfb], [B_k[g][b] for b in kblocks])
                t0 = tt * TT
                if t0 + TT > L - W:
                    lo = max(t0, L - W)

                    def dstk(j, cw, g=g, W=W, t0=t0):
                        r0 = t0 + j * 128 - (L - W)
                        return o_cp[g][l, r0:r0 + cw, 0, :]
                    if lo == t0:
                        out_tokmajor(nf[:, 0:w], nfb, w, dstk)
                    else:
                        out_tokmajor(nf[:, lo - t0:w], nfb, w - (lo - t0), lambda j, cw, g=g, W=W, lo=lo: o_cp[g][l, lo + j * 128 - (L - W): lo + j * 128 - (L - W) + cw, 0, :])
            zp, zb = zchunk(4 + 3 * g)
            if sample:
                vf, vfb = VSf[g]
                evac(zp, zb, vf[:, 0:w], vfb)
                T.dma(lambda e, vf=vf, g=g, W=W: e.dma_start(out=o_cs[g][l, :, W - 1, 1, :].rearrange("i d -> d i"), in_=vf[:, 0:NS], allow_slow_non_contiguous=True),
                      sl_out, reads=[vfb])
            else:
                nf, nfb = NF.get()
                evac(zp, zb, nf[:, 0:w], nfb)
                na = TT // dil
                vblocks = blocks_of(g, tt)
                if g < 2:
                    vd, vdb = VD.get()
                    T.op("act", lambda e, nf=nf, vd=vd, dil=dil: e.copy(out=deint(vd[:, 0:w], dil), in_=nat(nf[:, 0:w], dil)), [nfb], [vdb])

                    def f(e, vd=vd):
                        r = None
                        for i in range(4):
                            r = e.transpose(out=TPb[:, i * 128:(i + 1) * 128], in_=vd[:, i * 128:(i + 1) * 128], identity=ident_b[:])
                        return r
                    T.op("pe", f, [vdb, B_idb], [B_TP])
                    b0 = vblocks[0]
                    if g == 0:
                        T.op("dve", lambda e, b0=b0, g=g: e.tensor_copy(out=Vb[g][:, b0:b0 + 4, :], in_=TPb[:, 0:512].rearrange("p (i d) -> p i d", i=4)), [B_TP], [B_v[g][b] for b in vblocks])
                    else:
                        T.op("dve", lambda e, g=g: e.tensor_copy(out=Vb[g][:].rearrange("p (r m) d -> p r m d", r=4)[:, :, tt % 2, :],
                                                              in_=TPb[:, 0:512].rearrange("p (i d) -> p i d", i=4)), [B_TP], [B_v[g][b] for b in vblocks])
                else:
                    i0 = tt * na
                    vview = vT2[:].rearrange("p (r i) -> p i r", r=dil)[:, i0:i0 + na, :]
                    T.op("act", lambda e, nf=nf, vview=vview, dil=dil: e.copy(out=vview, in_=nat(nf[:, 0:w], dil)), [nfb], [B_vT2])
                    m = tt // 4
                    for q in range(4):
                        def f(e, q=q, m=m):
                            r = None
                            for i in range(4):
                                rr = q * 4 + i
                                r = e.transpose(out=TPb[:, i * 128:(i + 1) * 128], in_=vT2[:, rr * 256 + m * 128: rr * 256 + (m + 1) * 128], identity=ident_b[:])
                            return r
                        T.op("pe", f, [B_vT2, B_idb], [B_TP])
                        T.op("dve", lambda e, q=q, m=m: e.tensor_copy(out=Vb[2][:].rearrange("p (r m) d -> p r m d", r=16)[:, q * 4:(q + 1) * 4, m, :],
                                                                   in_=TPb[:, 0:512].rearrange("p (i d) -> p i d", i=4)), [B_TP], [B_v[2][(q * 4 + i) * 2 + m] for i in range(4)])
                t0 = tt * TT
                if t0 + TT > L - W:
                    lo = max(t0, L - W)
                    out_tokmajor(nf[:, lo - t0:w], nfb, w - (lo - t0), lambda j, cw, g=g, W=W, lo=lo: o_cp[g][l, lo + j * 128 - (L - W): lo + j * 128 - (L - W) + cw, 1, :])
        zp, zb = zchunk(12)
        zt, ztb = ZT.get()
        evac(zp, zb, zt[:, 0:w], ztb)
        rq, rqb = norm_rstd(zt[:, 0:w], ztb, w, 128)
        qm, qmb = QM
        T.op("dve", lambda e, zt=zt, rq=rq: e.scalar_tensor_tensor(out=qm[:, 0:w], in0=zt[:, 0:w], scalar=vcol(vb + V_MQ), in1=rq[:, 0:w], op0=ALU.mult, op1=ALU.mult),
             [ztb, rqb, B_vecs], [qmb])
        yb, ybb = YB.get()
        if sample:
            sample_attn(l, yb, ybb)
        else:
            prompt_attn(l, tt, yb, ybb)
        if debug and l == 0 and tt == 0:
            T.dma(lambda e: e.dma_start(out=dbg_y[:, :, :], in_=yb[:, :, :]), sl_out, reads=[ybb])
        T.dma(lambda e: e.dma_start(out=yg_in[l][st].rearrange("(j p) t -> p j t", p=128)[:, :, off:off + w], in_=yb[:, :, 0:w]), sl_y, reads=[ybb], writes=[B_ygi[l][st]])
        if sample or tt % 2 == 1:
            ag(yg_in[l][st], yg_out[l][st], B_ygi[l][st], B_ygo[l][st], sl_ccy)

    def blocks_of(g, tt):
        if g == 0:
            return [(4 * tt + i) % 8 for i in range(4)]
        if g == 1:
            return [r * 2 + tt % 2 for r in range(4)]
        return [r * 2 + tt // 4 for r in range(16)]

    def kblock_ap(g, blk):
        return kT[g][:, blk * 128:(blk + 1) * 128]

    def finish_attn(w, gate, gateb, yb, ybb, yj, dbg=False):
        rq, rqb = RQ.get()
        T.op("act", lambda e: e.activation(out=rq[:, 0:w], in_=ZZ[:, 0:w], func=AF.Ln), [B_ZZ], [rqb])
        T.op("act", lambda e: e.activation(out=rq[:, 0:w], in_=rq[:, 0:w], func=AF.Exp, scale=-1.0), [rqb], [rqb])
        nf, nfb = NF.get()
        T.op("dve", lambda e: e.tensor_tensor(out=nf[:, 0:w], in0=OP[:, 0:w], in1=rq[:, 0:w], op=ALU.mult), [B_OP, rqb], [nfb])
        T.op("dve", lambda e: e.tensor_tensor(out=yb[:, yj, 0:w], in0=nf[:, 0:w], in1=gate[:, 0:w], op=ALU.mult), [nfb, gateb], [ybb])
        if dbg:
            T.dma(lambda e: e.dma_start(out=dbg_od[:, :], in_=nf[:, 0:512]), sl_out, reads=[nfb])
            T.dma(lambda e: e.dma_start(out=dbg_rz[:, :], in_=rq[:, 0:512]), sl_out, reads=[rqb])

    def prompt_attn(l, tt, yb, ybb):
        vb = l * VL
        w = TT
        first = [True]
        for g in range(3):
            dil = DILS[g]
            nseg = 4 if g < 2 else 16
            sw_ = TT // nseg
            for typ in (0, 1):
                segs = []
                nvalid = 0
                for i in range(nseg):
                    if g == 0:
                        kb = 4 * tt + i - (1 - typ)
                        ok = kb >= 0
                        kb = kb % 8
                    elif g == 1:
                        m = tt - (1 - typ)
                        kb = i * 2 + m % 2
                        ok = m >= 0
                    else:
                        m = tt // 4 - (1 - typ)
                        kb = i * 2 + m % 2
                        ok = m >= 0
                    nvalid += ok
                    segs.append((i, kb))
                c_lo, c_hi = segs[0][0] * sw_, (segs[-1][0] + 1) * sw_
                sp_, spb = SP.get()
                qd, qdb = QD[g]

                def fs(e, segs=segs, sp_=sp_, qd=qd, g=g, sw_=sw_):
                    r = None
                    for (i, kb) in segs:
                        r = e.matmul(sp_[:, i * sw_:(i + 1) * sw_], lhsT=kblock_ap(g, kb), rhs=qd[:, i * sw_:(i + 1) * sw_], start=True, stop=True, skip_group_check=True)
                    return r
                T.op("pe", fs, [qdb] + [B_k[g][kb] for (_, kb) in segs], [spb])
                pt, ptb = PT.get()
                T.op("act", lambda e, sp_=sp_, pt=pt, c_lo=c_lo, c_hi=c_hi: e.activation(out=pt[:, c_lo:c_hi], in_=sp_[:, c_lo:c_hi], func=AF.Exp, scale=SCALE), [spb], [ptb])
                if g < 2:
                    mi = typ
                else:
                    mi = 2 + (tt % 4) * 2 + typ
                if nvalid == 0:
                    mi = 10
                elif nvalid < nseg:
                    mi = 11
                T.op("pool", lambda e, pt=pt, mi=mi, c_lo=c_lo, c_hi=c_hi: e.tensor_tensor(out=pt[:, c_lo:c_hi], in0=pt[:, c_lo:c_hi], in1=masks[:, mi * 512 + c_lo: mi * 512 + c_hi], op=ALU.mult),
                     [ptb, B_masks], [ptb])

                def fo(e, segs=segs, pt=pt, g=g, sw_=sw_, dil=dil, fst=first[0]):
                    r = None
                    k = 0
                    for (i, kb) in segs:
                        if g == 0:
                            oo, zz = OP[:, i * 128:(i + 1) * 128], ZZ[:, i * 128:(i + 1) * 128]
                        else:
                            oo, zz = OP[:, i:TT:dil], ZZ[:, i:TT:dil]
                        e.matmul(oo, lhsT=Vb[g][:, kb, :], rhs=pt[:, i * sw_:(i + 1) * sw_], start=(fst and k == 0), stop=False, skip_group_check=True)
                        r = e.matmul(zz, lhsT=ones_b[:], rhs=pt[:, i * sw_:(i + 1) * sw_], start=(fst and k == 0), stop=False, skip_group_check=True)
                        k += 1
                    return r
                T.op("pe", fo, [ptb, B_ones] + [B_v[g][kb] for (_, kb) in segs], [B_OP, B_ZZ])
                first[0] = False
        finish_attn(w, gates[1][0], gates[1][1], yb, ybb, 1, dbg=(debug and l == 0 and tt == 0))
        if debug and l == 0 and tt == 0:
            for g in range(3):
                T.dma(lambda e, g=g: e.dma_start(out=dbg_q[:, g, :], in_=QD[g][0][:, 0:512]), sl_out, reads=[QD[g][1]])
        qm, qmb = QM
        for kb in range(2):
            sp_, spb = SP.get()
            T.op("pe", lambda e, sp_=sp_, kb=kb: e.matmul(sp_[:, 0:w], lhsT=KmT[:, kb * 128:(kb + 1) * 128], rhs=qm[:, 0:w], start=True, stop=True), [B_KmT, qmb], [spb])
            pt, ptb = PT.get()
            T.op("act", lambda e, sp_=sp_, pt=pt: e.activation(out=pt[:, 0:w], in_=sp_[:, 0:w], func=AF.Exp, scale=SCALE), [spb], [ptb])

            def fo(e, pt=pt, kb=kb):
                e.matmul(OP[:, 0:w], lhsT=Vm[:, kb, :], rhs=pt[:, 0:w], start=(kb == 0), stop=(kb == 1), skip_group_check=True)
                return e.matmul(ZZ[:, 0:w], lhsT=ones_b[:], rhs=pt[:, 0:w], start=(kb == 0), stop=(kb == 1), skip_group_check=True)
            T.op("pe", fo, [ptb, B_ones, B_Vm], [B_OP, B_ZZ])
        finish_attn(w, gates[2][0], gates[2][1], yb, ybb, 2)
        pa, pab = PA
        pb_, pbb = PB
        pacc, paccb = PACC
        chain = [(ubuf, B_u, pa, pab, 1, 2), (pa, pab, pb_, pbb, 2, 4), (pb_, pbb, pa, pab, 4, 8), (pa, pab, pb_, pbb, 8, 16)]
        for k, (src, srcb, dst, dstb, sh, lo) in enumerate(chain):
            T.op("pool", lambda e, src=src, dst=dst, sh=sh, lo=lo: e.tensor_tensor(out=dst[:, lo:16 + TT], in0=src[:, lo:16 + TT], in1=src[:, lo - sh:16 + TT - sh], op=ALU.add), [srcb], [dstb])
            if k == 0:
                T.op("dve", lambda e, dst=dst: e.tensor_scalar(out=pacc[:], in0=dst[:, 16:16 + TT], scalar1=vcol(V_SEL + 0), scalar2=None, op0=ALU.mult), [dstb, B_vecs], [paccb])
            else:
                T.op("dve", lambda e, dst=dst, k=k: e.scalar_tensor_tensor(out=pacc[:], in0=dst[:, 16:16 + TT], scalar=vcol(V_SEL + k), in1=pacc[:], op0=ALU.mult, op1=ALU.add),
                     [dstb, B_vecs, paccb], [paccb])
        dtb, dtbb = DTb
        if tt == 0:
            nf, nfb = NF.get()
            T.op("dve", lambda e, nf=nf: e.tensor_scalar(out=nf[:, 16:TT], in0=pacc[:, 16:TT], scalar1=vcol(V_IW), scalar2=None, op0=ALU.mult), [paccb, B_vecs], [nfb])
            T.op("dve", lambda e, nf=nf: e.tensor_tensor(out=nf[:, 0:16], in0=pacc[:, 0:16], in1=vcol(V_CI, 16), op=ALU.mult), [paccb, B_vecs], [nfb])
            T.op("dve", lambda e, nf=nf: e.tensor_tensor(out=dtb[:], in0=nf[:], in1=ubuf[:, 16:16 + TT], op=ALU.subtract), [nfb, B_u], [dtbb])
        else:
            T.op("dve", lambda e: e.scalar_tensor_tensor(out=dtb[:], in0=pacc[:], scalar=vcol(V_IW), in1=ubuf[:, 16:16 + TT], op0=ALU.mult, op1=ALU.subtract),
                 [paccb, B_vecs, B_u], [dtbb])
        zp, zb = ZP.get()
        T.op("pe", lambda e, zp=zp: e.matmul(zp[:, 0:w], lhsT=pw[:], rhs=dtb[:], start=True, stop=True), [B_pw, dtbb], [zb])
        T.op("dve", lambda e, zp=zp: e.scalar_tensor_tensor(out=yb[:, 0, 0:w], in0=zp[:, 0:w], scalar=vcol(vb + V_PS), in1=gates[0][0][:, 0:w], op0=ALU.mult, op1=ALU.mult),
             [zb, B_vecs, gates[0][1]], [ybb])
        if tt == NTT - 1:
            T.dma(lambda e: e.dma_start(out=o_spp[l, :, :].rearrange("r c -> c r"), in_=ubuf[:, 16 + TT - 15:16 + TT], allow_slow_non_contiguous=True), sl_out, reads=[B_u])
        T.op("pool", lambda e: e.tensor_copy(out=ubuf[:, 0:16], in_=ubuf[:, TT:TT + 16]), [B_u], [B_u])

    def sample_attn(l, yb, ybb):
        vb = l * VL
        w = SW
        T.op("dve", lambda e: e.memset(OSF[:], 0.0), [], [B_OSF])
        first = [True]
        for i in range(NS):
            for g in range(4):
                kc, kcb = KC.get()
                kct, kctb = KCT.get()
                vcb, vcbb = VCb.get()
                nkb = 1 if g < 3 else 2
                if g < 3:
                    W, dil = WINS[g], DILS[g]
                    T.dma(lambda e, kc=kc, g=g, i=i, W=W, dil=dil: e.dma_start(out=kc[:, 0, :, :], in_=d_ck[g][l, i, 0:W:dil, :, :]), sl_kc, writes=[kcb])
                else:
                    T.dma(lambda e, kc=kc, i=i: e.dma_start(out=kc[:], in_=d_cm[l, i, :, :, :].rearrange("(kb p) j d -> p kb j d", p=128)), sl_kc, writes=[kcb])
                for kb in range(nkb):
                    T.op("pe", lambda e, kc=kc, kb=kb: e.transpose(out=TP[:, 0:128], in_=kc[:, kb, 0, :], identity=ident_f[:]), [kcb, B_idf], [B_TP])
                    T.op("dve", lambda e, kct=kct, kb=kb: e.tensor_copy(out=kct[:, kb * 128:(kb + 1) * 128], in_=TP[:, 0:128]), [B_TP], [kctb])
                    T.op("act", lambda e, kc=kc, vcb=vcb, kb=kb: e.copy(out=vcb[:, kb, :], in_=kc[:, kb, 1, :]), [kcb], [vcbb])
                q_t, q_b = QD[g] if g < 3 else QM
                sp_, spb = SP.get()

                def fs(e, sp_=sp_, kct=kct, q_t=q_t, i=i, g=g, nkb=nkb):
                    r = None
                    for kb in range(nkb):
                        r = e.matmul(sp_[:, kb:kb + 1], lhsT=kct[:, kb * 128:(kb + 1) * 128], rhs=q_t[:, i:i + 1], start=True, stop=True, skip_group_check=True)
                    if g < 3:
                        r = e.matmul(sp_[0:1, 2:3], lhsT=KSb[g][0][:, i:i + 1], rhs=q_t[:, i:i + 1], start=True, stop=True, skip_group_check=True)
                    return r
                T.op("pe", fs, [kctb, q_b] + ([KSb[g][1]] if g < 3 else []), [spb])
                pss, pssb = PSs.get()
                T.op("act", lambda e, sp_=sp_, pss=pss, nkb=nkb: e.activation(out=pss[:, 0:nkb], in_=sp_[:, 0:nkb], func=AF.Exp, scale=SCALE), [spb], [pssb])
                if g < 3:
                    T.op("act", lambda e, sp_=sp_, pss=pss: e.activation(out=pss[0:1, 2:3], in_=sp_[0:1, 2:3], func=AF.Exp, scale=SCALE), [spb], [pssb])
                col = i if g < 3 else 8 + i
                st_flag = first[0]

                def fo(e, pss=pss, vcb=vcb, col=col, g=g, i=i, nkb=nkb, st_flag=st_flag):
                    r = None
                    for kb in range(nkb):
                        e.matmul(OP[:, col:col + 1], lhsT=vcb[:, kb, :], rhs=pss[:, kb:kb + 1], start=(st_flag and kb == 0), stop=False, skip_group_check=True)
                        r = e.matmul(ZZ[:, col:col + 1], lhsT=ones_b[:], rhs=pss[:, kb:kb + 1], start=(st_flag and kb == 0), stop=False, skip_group_check=True)
                    if g < 3:
                        e.matmul(ZZ[:, col:col + 1], lhsT=ones_b[0:1, :], rhs=pss[0:1, 2:3], start=False, stop=False, skip_group_check=True)
                        r = e.matmul(NQ[:, g * 4 + i:g * 4 + i + 1], lhsT=ones_b[0:1, :], rhs=pss[0:1, 2:3], start=True, stop=True, skip_group_check=True)
                    return r
                T.op("pe", fo, [pssb, vcbb, B_ones], [B_OP, B_ZZ, B_NQ])
                first[0] = False
                if g < 3:
                    T.op("dve", lambda e, g=g, i=i: e.scalar_tensor_tensor(out=OSF[:, i:i + 1], in0=VSf[g][0][:, i:i + 1], scalar=NQ[:, g * 4 + i:g * 4 + i + 1], in1=OSF[:, i:i + 1],
                                                                      op0=ALU.mult, op1=ALU.add), [B_NQ, VSf[g][1], B_OSF], [B_OSF])
        rq, rqb = RQ.get()
        T.op("act", lambda e: e.activation(out=rq[:, 0:16], in_=ZZ[:, 0:16], func=AF.Ln), [B_ZZ], [rqb])
        T.op("act", lambda e: e.activation(out=rq[:, 0:16], in_=rq[:, 0:16], func=AF.Exp, scale=-1.0), [rqb], [rqb])
        T.op("dve", lambda e: e.memset(yb[:].rearrange("p j t -> p (j t)"), 0.0), [], [ybb])
        nf, nfb = NF.get()
        T.op("dve", lambda e: e.tensor_tensor(out=nf[:, 0:NS], in0=OP[:, 0:NS], in1=OSF[:, 0:NS], op=ALU.add), [B_OP, B_OSF], [nfb])
        T.op("dve", lambda e: e.tensor_tensor(out=nf[:, 0:NS], in0=nf[:, 0:NS], in1=rq[:, 0:NS], op=ALU.mult), [nfb, rqb], [nfb])
        T.op("dve", lambda e: e.tensor_tensor(out=yb[:, 1, 0:NS], in0=nf[:, 0:NS], in1=gates[1][0][:, 0:NS], op=ALU.mult), [nfb, gates[1][1]], [ybb])
        T.op("dve", lambda e: e.tensor_tensor(out=nf[:, 8:8 + NS], in0=OP[:, 8:8 + NS], in1=rq[:, 8:8 + NS], op=ALU.mult), [B_OP, rqb], [nfb])
        T.op("dve", lambda e: e.tensor_tensor(out=yb[:, 2, 0:NS], in0=nf[:, 8:8 + NS], in1=gates[2][0][:, 0:NS], op=ALU.mult), [nfb, gates[2][1]], [ybb])
        for i in range(NS):
            T.dma(lambda e, i=i: e.dma_start(out=UE[:, i, 0:15], in_=d_sp[l, i, :, :].rearrange("r c -> c r"), allow_slow_non_contiguous=True), sl_kc, writes=[B_UE])
        T.op("dve", lambda e: e.tensor_copy(out=UE[:, :, 15], in_=ubuf[:, 16:16 + NS]), [B_u, B_UE], [B_UE])
        for i in range(NS):
            T.op("dve", lambda e, i=i: e.tensor_tensor(out=UP[:, i, :], in0=UE[:, i, :], in1=vcol(V_WM, 16), op=ALU.mult), [B_UE, B_vecs], [B_UP])
        T.op("dve", lambda e: e.reduce_sum(out=USUM[:], in_=UP[:], axis=AX.X), [B_UP], [B_US])
        dtb, dtbb = DTb
        T.op("dve", lambda e: e.memset(dtb[:, 0:SW], 0.0), [], [dtbb])
        T.op("dve", lambda e: e.tensor_tensor(out=dtb[:, 0:NS], in0=USUM[:], in1=ubuf[:, 16:16 + NS], op=ALU.subtract), [B_US, B_u], [dtbb])
        zp, zb = ZP.get()
        T.op("pe", lambda e, zp=zp: e.matmul(zp[:, 0:w], lhsT=pw[:], rhs=dtb[:, 0:w], start=True, stop=True), [B_pw, dtbb], [zb])
        T.op("dve", lambda e, zp=zp: e.scalar_tensor_tensor(out=yb[:, 0, 0:NS], in0=zp[:, 0:NS], scalar=vcol(vb + V_PS), in1=gates[0][0][:, 0:NS], op0=ALU.mult, op1=ALU.mult),
             [zb, B_vecs, gates[0][1]], [ybb])
        for i in range(NS):
            T.dma(lambda e, i=i: e.dma_start(out=o_sps[l, i, :, :].rearrange("r c -> c r"), in_=UE[:, i, 1:16], allow_slow_non_contiguous=True), sl_out, reads=[B_UE])
        T.op("pool", lambda e: e.memset(ubuf[:, 0:16], 0.0), [], [B_u])

    def phase2(l, tt):
        sample = (tt == NTT)
        w = SW if sample else TT
        st = 4 if sample else tt // 2
        off = 0 if sample else (tt % 2) * TT
        xoff = L if sample else tt * TT
        last = (l == depth - 1)
        T.dma(lambda e: e.dma_start(out=YT[:, :, 0:w], in_=yg_out[l][st].rearrange("(k p) t -> p k t", p=128)[:, :, off:off + w]), sl_y, reads=[B_ygo[l][st]], writes=[B_YT])
        T.dma(lambda e: e.dma_start(out=XR[:, :, 0:w], in_=xres_v[:, :, xoff:xoff + w]), sl_x, reads=[B_xres[tt]], writes=[B_XR])
        for c in range(4):
            zp, zb = ZP.get()

            def f(e, zp=zp, c=c):
                r = None
                for k in range(12):
                    r = e.matmul(zp[:, 0:w], lhsT=wo[:, k, c * 128:(c + 1) * 128], rhs=YT[:, k, 0:w], start=(k == 0), stop=(k == 11))
                return r
            T.op("pe", f, [B_wo, B_YT], [zb])
            T.op("dve", lambda e, zp=zp, c=c: e.tensor_tensor(out=XR[:, c, 0:w], in0=zp[:, 0:w], in1=XR[:, c, 0:w], op=ALU.add), [zb, B_XR], [B_XR])
        if not last:
            T.dma(lambda e: e.dma_start(out=xres_v[:, :, xoff:xoff + w], in_=XR[:, :, 0:w]), sl_x, reads=[B_XR], writes=[B_xres[tt]])
            T.op("act", lambda e: e.copy(out=XB[:, :, 0:w], in_=XR[:, :, 0:w]), [B_XR], [B_XB])
            T.dma(lambda e: e.dma_start(out=xg_in[l + 1][st].rearrange("(c p) t -> p c t", p=128)[:, :, off:off + w], in_=XB[:, :, 0:w]), sl_x,
                  reads=[B_XB], writes=[B_xgi[l + 1][st]])
            if sample or tt % 2 == 1:
                ag(xg_in[l + 1][st], xg_out[l + 1][st], B_xgi[l + 1][st], B_xgo[l + 1][st])
        else:
            nb = 1 if sample else 4
            for j in range(nb):
                cw = NS if sample else 128

                def f(e, j=j, cw=cw):
                    r = None
                    for c in range(4):
                        r = e.transpose(out=TP[0:cw, c * 128:(c + 1) * 128], in_=XR[:, c, j * 128:j * 128 + cw], identity=ident_f[:])
                    return r
                T.op("pe", f, [B_XR, B_idf], [B_TP])
                ct, cb = CST.get()
                T.op("act", lambda e, ct=ct, cw=cw: e.copy(out=ct[0:cw, :], in_=TP[0:cw, :]), [B_TP], [cb])
                if sample:
                    T.dma(lambda e, ct=ct: e.dma_start(out=o_ys[:, :], in_=ct[0:NS, :]), sl_out, reads=[cb])
                else:
                    T.dma(lambda e, ct=ct, j=j: e.dma_start(out=o_yp[tt * TT + j * 128: tt * TT + (j + 1) * 128, :], in_=ct[:, :]), sl_out, reads=[cb])

    load_w_out(0)
    for l in range(depth):
        load_w_in(l)
        mem_kv(l)
        order = [("1", 0), ("1", 1), ("1", 2), ("1", 3), ("2", 0), ("2", 1), ("1", 4), ("1", 5), ("2", 2), ("2", 3),
                 ("1", 6), ("1", 7), ("1", 8), ("2", 4), ("2", 5), ("2", 6), ("2", 7), ("2", 8)]
        for ph, tt in order:
            if ph == "1":
                phase1(l, tt)
            else:
                phase2(l, tt)
        if l + 1 < depth:
            load_w_out(l + 1)

    with nc.Block() as block:
        @block.tensor
        def _(h):
            T.replay("pe", h)

        @block.scalar
        def _(h):
            T.replay("act", h)

        @block.vector
        def _(h):
            T.replay("dve", h)

        @block.gpsimd
        def _(h):
            T.replay("pool", h)

        @block.sync
        def _(h):
            T.replay("sp", h, final=True)
    es.close()
    return nc


def _slot_cols(s):
    cols = []
    cols += list(range(s * 128, s * 128 + 128))
    cols += list(range(512 + s * 128, 512 + s * 128 + 128))
    c0 = 1024
    for g in range(3):
        for j in range(3):
            base = c0 + ((g * 3 + j) * 4 + s) * 128
            cols += list(range(base, base + 128))
    c1 = c0 + 9 * 512
    cols += list(range(c1 + s * 128, c1 + s * 128 + 128))
    c2 = c1 + 512
    cols += list(range(c2 + s * 128, c2 + s * 128 + 128))
    c3 = c2 + 512
    cols += list(range(c3 + s * 128, c3 + s * 128 + 128))
    return np.array(cols)


def _masks():
    kb = np.arange(128)[:, None]
    a = np.arange(128)[None, :]
    cur = (kb <= a).astype(np.float32)
    prev = (a <= kb).astype(np.float32)
    out = [np.tile(prev, (1, 4)), np.tile(cur, (1, 4))]
    a32 = np.arange(32)[None, :]
    for j in range(4):
        q = 32 * j + a32
        out.append(np.tile((q <= kb).astype(np.float32), (1, 16)))
        out.append(np.tile((kb <= q).astype(np.float32), (1, 16)))
    out.append(np.zeros((128, 512), np.float32))
    p0 = np.tile(prev, (1, 4))
    p0[:, 0:128] = 0.0
    out.append(p0)
    return np.concatenate(out, axis=1)


_NC_CACHE = {}


def kernel(x_prompt, x_sample, state_pool, cache_dil_w128, cache_dil_w512, cache_dil_w2048,
           cache_mem_kv, mem_prompt, norm_g, w_in, pool_w, pool_scale, dil_q_norm, dil_k_norm,
           mem_norm_g, w_mem_kv, mem_q_norm, mem_k_norm, w_out, _depth=DEPTH, _debug=False):
    f = lambda a: np.ascontiguousarray(np.asarray(a, dtype=np.float32))
    x_prompt, x_sample, state_pool = f(x_prompt), f(x_sample), f(state_pool)
    caches = [f(cache_dil_w128), f(cache_dil_w512), f(cache_dil_w2048)]
    cache_mem_kv, mem_prompt, norm_g, w_in, pool_w = f(cache_mem_kv), f(mem_prompt), f(norm_g), f(w_in), f(pool_w)
    pool_scale, dil_q_norm, dil_k_norm, mem_norm_g = f(pool_scale), f(dil_q_norm), f(dil_k_norm), f(mem_norm_g)
    w_mem_kv, mem_q_norm, mem_k_norm, w_out = f(w_mem_kv), f(mem_q_norm), f(mem_k_norm), f(w_out)
    depth = _depth
    if (depth, _debug) not in _NC_CACHE:
        _NC_CACHE[(depth, _debug)] = build(depth, _debug)
    nc = _NC_CACHE[(depth, _debug)]
    ident = np.eye(128, dtype=np.float32)
    masks = _masks()
    in_maps = []
    for c in range(8):
        b, s = c // 4, c % 4
        w = 2 ** (s + 1)
        vecs = np.zeros((128, NV), np.float32)
        for l in range(DEPTH):
            vb = l * VL
            vecs[:, vb + V_NG: vb + V_NG + 16] = norm_g[l].reshape(16, 128).T
            vecs[:, vb + V_MG: vb + V_MG + 16] = mem_norm_g[l].reshape(16, 128).T
            vecs[:, vb + V_PS] = pool_scale[l, s * 128:(s + 1) * 128]
            for g in range(3):
                vecs[:, vb + V_QN + g] = dil_q_norm[l, g]
                vecs[:, vb + V_KN + g] = dil_k_norm[l, g]
            vecs[:, vb + V_MQ] = mem_q_norm[l]
        vecs[:, V_SEL + s] = 1.0
        wm = np.zeros(16, np.float32)
        wm[16 - w:] = 1.0 / w
        vecs[:, V_WM:V_WM + 16] = wm[None, :]
        vecs[:, V_CI:V_CI + 16] = (1.0 / np.minimum(w, np.arange(16) + 1.0))[None, :]
        vecs[:, V_IW] = 1.0 / w
        rows = np.concatenate([np.arange(br * 512 + r * 128, br * 512 + r * 128 + 128) for r in range(4) for br in range(3)])
        m = {
            "xin": np.ascontiguousarray(x_prompt[b][:, s * 512:(s + 1) * 512]),
            "xs": np.ascontiguousarray(x_sample[4 * b:4 * b + 4, 0, s * 512:(s + 1) * 512]),
            "vecs": vecs,
            "mkn": np.ascontiguousarray(np.broadcast_to(mem_k_norm.reshape(1, DEPTH * 128), (128, DEPTH * 128))),
            "ident": ident,
            "masks": masks,
            "win": np.ascontiguousarray(w_in[:, :, _slot_cols(s)]),
            "wout": np.ascontiguousarray(w_out[:, rows][:, :, s * 512:(s + 1) * 512]),
            "wm": np.ascontiguousarray(np.concatenate([w_mem_kv[:, :, s * 128:(s + 1) * 128], w_mem_kv[:, :, 512 + s * 128:512 + (s + 1) * 128]], axis=2)),
            "pw": np.ascontiguousarray(pool_w[:, s]),
            "mem": np.ascontiguousarray(mem_prompt[b]),
            "cm": np.ascontiguousarray(cache_mem_kv[:, 4 * b:4 * b + 4, :, :, s, :]),
            "spool": np.ascontiguousarray(state_pool[:, 4 * b:4 * b + 4, :, s * 128:(s + 1) * 128]),
        }
        for g in range(3):
            m[f"ck{g}"] = np.ascontiguousarray(caches[g][:, 4 * b:4 * b + 4, :, :, s, :])
        in_maps.append(m)
    res = run_bass_kernel_spmd(nc, in_maps, core_ids=list(range(8)))
    R = res.results
    y_prompt = np.zeros((2, L, D), np.float32)
    y_sample = np.zeros((8, 1, D), np.float32)
    sp_p = np.zeros((DEPTH, 2, 15, 512), np.float32)
    cp = [np.zeros((DEPTH, 2, W, 2, 4, 128), np.float32) for W in WINS]
    cmp_ = np.zeros((DEPTH, 2, 256, 2, 4, 128), np.float32)
    sp_s = np.zeros((DEPTH, 8, 15, 512), np.float32)
    cs = [np.zeros((DEPTH, 8, W, 2, 4, 128), np.float32) for W in WINS]
    for c in range(8):
        b, s = c // 4, c % 4
        r = R[c]
        y_prompt[b][:, s * 512:(s + 1) * 512] = r["o_yp"]
        y_sample[4 * b:4 * b + 4, 0, s * 512:(s + 1) * 512] = r["o_ys"]
        sp_p[:, b, :, s * 128:(s + 1) * 128] = r["o_spp"]
        cmp_[:, b, :, :, s, :] = r["o_cmp"]
        sp_s[:, 4 * b:4 * b + 4, :, s * 128:(s + 1) * 128] = r["o_sps"]
        for g in range(3):
            cp[g][:, b, :, :, s, :] = r[f"o_cp{g}"]
            cs[g][:, 4 * b:4 * b + 4, :, :, s, :] = r[f"o_cs{g}"]
    return (y_prompt, y_sample, sp_p, cp[0], cp[1], cp[2], cmp_, sp_s, cs[0], cs[1], cs[2])
```
